# Optimizing a Trainium2 kernel written in Bass

```python
import math
import jax, jax.numpy as jnp
from jax import lax
import numpy as np

D_MODEL = 1024
BATCH = 16
SEQ = 2048
DEPTH = 2

CHUNK = 64
D_PLE = 256
EPS = 1e-6

A_HEADS = 8
A_HEAD_DIM = 64
A_WIDTH = A_HEADS * A_HEAD_DIM
A_LEFT_CHUNKS = 8
A_BAND = (A_LEFT_CHUNKS + 1) * CHUNK
REL_MAX = 128

B_HEADS = 4
B_NOPE = 128
B_ROPE = 64
B_VDIM = 128
B_WIDTH = B_HEADS * B_VDIM
Q_LORA = 256
KV_LORA = 128
ROPE_THETA = 10000.0
Q_BLOCK = 128
MAX_POS_OFFSET = 4096

D_MIX = A_WIDTH + B_WIDTH
IN_SIZES = (A_WIDTH, A_WIDTH, A_WIDTH, A_WIDTH, Q_LORA, KV_LORA, B_ROPE, B_WIDTH)
D_IN = 4 * A_WIDTH + Q_LORA + KV_LORA + B_ROPE + B_WIDTH

kernel_name = "hybrid_chunked_relpos_mla_trunk"


def rmsnorm(x, g):
    xf = x.astype(jnp.float32)
    y = xf * lax.rsqrt(jnp.mean(xf * xf, axis=-1, keepdims=True) + EPS)
    return (y * g.astype(jnp.float32)).astype(x.dtype)


def split_cols(z, sizes):
    idx = np.cumsum(np.array(sizes[:-1])).tolist()
    return jnp.split(z, idx, axis=-1)


def rope_tables(positions, dim):
    inv_freq = ROPE_THETA ** (-jnp.arange(0, dim, 2, dtype=jnp.float32) / dim)
    ang = positions.astype(jnp.float32)[..., None] * inv_freq
    return jnp.cos(ang), jnp.sin(ang)


def apply_rope(x, cos, sin):
    half = x.shape[-1] // 2
    x1 = x[..., :half].astype(jnp.float32)
    x2 = x[..., half:].astype(jnp.float32)
    out = jnp.concatenate([x1 * cos - x2 * sin, x2 * cos + x1 * sin], axis=-1)
    return out.astype(x.dtype)


def chunked_relpos_attention(q, k, v, rel_bias):
    B, S = q.shape[0], q.shape[1]
    nc = S // CHUNK
    qc = q.reshape(B, nc, CHUNK, A_HEADS, A_HEAD_DIM)
    pad = jnp.zeros((B, A_LEFT_CHUNKS * CHUNK, A_HEADS, A_HEAD_DIM), k.dtype)
    kc = jnp.concatenate([pad, k], axis=1).reshape(B, nc + A_LEFT_CHUNKS, CHUNK, A_HEADS, A_HEAD_DIM)
    vc = jnp.concatenate([pad, v], axis=1).reshape(B, nc + A_LEFT_CHUNKS, CHUNK, A_HEADS, A_HEAD_DIM)
    band = jnp.arange(nc)[:, None] + jnp.arange(A_LEFT_CHUNKS + 1)[None, :]
    kb = kc[:, band].reshape(B, nc, A_BAND, A_HEADS, A_HEAD_DIM)
    vb = vc[:, band].reshape(B, nc, A_BAND, A_HEADS, A_HEAD_DIM)
    s = jnp.einsum('bnqhd,bnkhd->bhnqk', qc, kb).astype(jnp.float32) * (A_HEAD_DIM ** -0.5)
    q_off = jnp.arange(CHUNK) + A_LEFT_CHUNKS * CHUNK
    k_off = jnp.arange(A_BAND)
    rel = jnp.clip(q_off[:, None] - k_off[None, :], -REL_MAX, REL_MAX) + REL_MAX
    bias = rel_bias.astype(jnp.float32)[:, rel]
    valid = jnp.repeat(band >= A_LEFT_CHUNKS, CHUNK, axis=1)
    s = jnp.where(valid[None, None, :, None, :], s + bias[:, None], -jnp.inf)
    pr = jax.nn.softmax(s, axis=-1).astype(v.dtype)
    o = jnp.einsum('bhnqk,bnkhd->bnqhd', pr, vb)
    return o.reshape(B, S, A_WIDTH)


def mla_attention(c_q, c_kv, k_rope_raw, g_q, w_uq, g_kv, w_ukv, cos, sin):
    B, S = c_q.shape[0], c_q.shape[1]
    q = (rmsnorm(c_q, g_q) @ w_uq).reshape(B, S, B_HEADS, B_NOPE + B_ROPE)
    q_nope, q_rope = q[..., :B_NOPE], q[..., B_NOPE:]
    q_rope = apply_rope(q_rope, cos[:, :, None, :], sin[:, :, None, :])
    kv = (rmsnorm(c_kv, g_kv) @ w_ukv).reshape(B, S, B_HEADS, B_NOPE + B_VDIM)
    k_nope, v = kv[..., :B_NOPE], kv[..., B_NOPE:]
    k_rope = apply_rope(k_rope_raw, cos, sin)
    scale = (B_NOPE + B_ROPE) ** -0.5
    nqb = S // Q_BLOCK
    qn = q_nope.reshape(B, nqb, Q_BLOCK, B_HEADS, B_NOPE).transpose(1, 0, 2, 3, 4)
    qr = q_rope.reshape(B, nqb, Q_BLOCK, B_HEADS, B_ROPE).transpose(1, 0, 2, 3, 4)
    key_chunk = jnp.arange(S) // CHUNK

    def block(args):
        qn_b, qr_b, qb = args
        s = (jnp.einsum('bqhd,bkhd->bhqk', qn_b, k_nope)
             + jnp.einsum('bqhd,bkd->bhqk', qr_b, k_rope)).astype(jnp.float32) * scale
        q_chunk = (qb * Q_BLOCK + jnp.arange(Q_BLOCK)) // CHUNK
        mask = key_chunk[None, :] <= q_chunk[:, None]
        s = jnp.where(mask[None, None], s, -jnp.inf)
        pr = jax.nn.softmax(s, axis=-1).astype(v.dtype)
        return jnp.einsum('bhqk,bkhd->bqhd', pr, v)

    o = lax.map(block, (qn, qr, jnp.arange(nqb)))
    return o.transpose(1, 0, 2, 3, 4).reshape(B, S, B_WIDTH)


def setup_inputs(seed: int = 0) -> dict:
    key = jax.random.key(seed)
    ks = jax.random.split(key, 20)
    f32 = jnp.float32
    nrm = lambda k, shape, fan_in: jax.random.normal(k, shape, f32) * (fan_in ** -0.5)
    gain = lambda k, shape: 1.0 + 0.05 * jax.random.normal(k, shape, f32)
    x = jax.random.normal(ks[0], (BATCH, SEQ, D_MODEL), f32)
    p = jax.random.normal(ks[1], (DEPTH, BATCH, SEQ, D_PLE), f32)
    offs = jax.random.randint(ks[2], (BATCH, 1), 0, MAX_POS_OFFSET, dtype=jnp.int32)
    positions = (jnp.arange(SEQ, dtype=jnp.int32)[None, :] + offs).astype(jnp.int32)
    return {
        "x": x,
        "p": p,
        "positions": positions,
        "norm_mix": gain(ks[3], (DEPTH, D_MODEL)),
        "w_in": nrm(ks[4], (DEPTH, D_MODEL, D_IN), D_MODEL),
        "rel_bias": 0.5 * jax.random.normal(ks[5], (DEPTH, A_HEADS, 2 * REL_MAX + 1), f32),
        "g_q": gain(ks[6], (DEPTH, Q_LORA)),
        "w_uq": nrm(ks[7], (DEPTH, Q_LORA, B_HEADS * (B_NOPE + B_ROPE)), Q_LORA),
        "g_kv": gain(ks[8], (DEPTH, KV_LORA)),
        "w_ukv": nrm(ks[9], (DEPTH, KV_LORA, B_HEADS * (B_NOPE + B_VDIM)), KV_LORA),
        "w_out": nrm(ks[10], (DEPTH, D_MIX, D_MODEL), D_MIX),
        "norm_ple": gain(ks[11], (DEPTH, D_MODEL)),
        "w_pe": nrm(ks[12], (DEPTH, D_PLE, D_MODEL), D_PLE),
        "w_pg": nrm(ks[13], (DEPTH, D_MODEL, D_MODEL), D_MODEL),
        "b_pg": 0.02 * jax.random.normal(ks[14], (DEPTH, D_MODEL), f32),
        "norm_final": gain(ks[15], (D_MODEL,)),
    }


def reference(x, p, positions, norm_mix, w_in, rel_bias, g_q, w_uq, g_kv, w_ukv, w_out,
              norm_ple, w_pe, w_pg, b_pg, norm_final):
    B, S = x.shape[0], x.shape[1]
    cos, sin = rope_tables(positions, B_ROPE)
    h = x
    for i in range(DEPTH):
        xn = rmsnorm(h, norm_mix[i])
        z = xn @ w_in[i]
        q_a, k_a, v_a, gate_a, c_q, c_kv, k_rope_raw, gate_b = split_cols(z, IN_SIZES)
        shp = (B, S, A_HEADS, A_HEAD_DIM)
        o_a = chunked_relpos_attention(q_a.reshape(shp), k_a.reshape(shp), v_a.reshape(shp), rel_bias[i])
        o_b = mla_attention(c_q, c_kv, k_rope_raw, g_q[i], w_uq[i], g_kv[i], w_ukv[i], cos, sin)
        mixed = jnp.concatenate([o_a * jax.nn.silu(gate_a), o_b * jax.nn.silu(gate_b)], axis=-1)
        h = h + mixed @ w_out[i]
        gate = jax.nn.sigmoid(rmsnorm(h, norm_ple[i]) @ w_pg[i] + b_pg[i])
        h = h + (p[i] @ w_pe[i]) * gate
    return rmsnorm(h, norm_final)
```

```python
import contextlib
import numpy as np
import concourse.bass as bass
import concourse.mybir as mybir
from concourse.bass_utils import run_bass_kernel_spmd

F32 = mybir.dt.float32
BF16 = mybir.dt.bfloat16
I32 = mybir.dt.int32
AF = mybir.ActivationFunctionType
ALU = mybir.AluOpType

NCORES = 8
SEQ = 2048
DM = 1024
NT = SEQ // 128
NSLAB = 4
SL = 512
EPS = 1e-6
SCALE_A = 64 ** -0.5
SCALE_B = 192 ** -0.5
TWO_PI_HI = 6.28125
TWO_PI_LO = 0.0019353071795864769


class Region:
    __slots__ = ("name", "w", "rs")

    def __init__(self, name, fence=None):
        self.name = name
        self.w = None
        self.rs = list(fence) if fence else []


class Sched:
    ENGS = ("pe", "act", "dve", "pool", "sp")

    def __init__(self, nc):
        self.nc = nc
        self.ops = {e: [] for e in self.ENGS}
        self.cnt = {}
        self.known = {e: {} for e in self.ENGS}
        self.sems = {}
        self.snap = {}
        self.nops = 0

    def sem(self, key):
        if key not in self.sems:
            self.sems[key] = self.nc.alloc_semaphore(f"s_{key}")
            self.cnt[key] = 0
        return self.sems[key]

    def _need(self, eng, deps):
        kn = self.known[eng]
        best = {}
        for (k, v) in deps:
            if kn.get(k, 0) >= v:
                continue
            if best.get(k, 0) < v:
                best[k] = v
        for k, v in best.items():
            kn[k] = v
            sn = self.snap.get((k, v))
            if sn:
                for kk, vv in sn.items():
                    if kn.get(kk, 0) < vv:
                        kn[kk] = vv
        return list(best.items())

    def op(self, eng, fn, reads=(), writes=(), inc=True, dma_key=None):
        deps = []
        for r in reads:
            if r.w is not None:
                deps.append(r.w)
        for w in writes:
            if w.w is not None and not (dma_key is not None and w.w[0] == dma_key):
                deps.append(w.w)
            deps.extend(w.rs)
        if eng == "pe":
            deps = [d for d in deps if d[0] != "pe"]
        waits = self._need(eng, deps)
        if dma_key is not None:
            key = dma_key
            self.sem(key)
            self.cnt[key] += 16
            val = self.cnt[key]
            incamt = 16
        else:
            key = eng
            self.sem(key)
            if inc:
                self.cnt[key] += 1
                val = self.cnt[key]
            else:
                val = self.cnt[key] + 1
            incamt = 1
        do_inc = inc or dma_key is not None
        self.ops[eng].append((fn, waits, (key, incamt) if do_inc else None))
        self.nops += 1
        tok = (key, val)
        sn = dict(self.known[eng])
        if dma_key is None:
            sn[key] = val
        self.snap[tok] = sn
        for r in reads:
            r.rs.append(tok)
        for w in writes:
            w.w = tok
            w.rs = []
        return tok

    def fence(self):
        return [(k, v) for k, v in self.cnt.items() if v > 0]

    def barrier(self):
        allk = [(k, v) for k, v in self.cnt.items() if v > 0]
        for e in self.ENGS:
            waits = self._need(e, allk)
            if waits:
                self.ops[e].append((None, waits, None))
        self.snap = {}

    def final_wait(self, eng):
        allk = [(k, v) for k, v in self.cnt.items() if v > 0]
        waits = self._need(eng, allk)
        self.ops[eng].append((None, waits, None))

    def emit(self):
        nc = self.nc
        handles = {"pe": "tensor", "act": "scalar", "dve": "vector", "pool": "gpsimd", "sp": "sync"}
        sems = self.sems
        with nc.Block() as block:
            for e in self.ENGS:
                ops = self.ops[e]

                def body(h, ops=ops):
                    for fn, waits, incd in ops:
                        for (k, v) in waits:
                            h.wait_ge(sems[k], v)
                        if fn is None:
                            continue
                        ins = fn(h)
                        if incd is not None:
                            ins.then_inc(sems[incd[0]], incd[1])
                getattr(block, handles[e])(body)


class Rot:
    def __init__(self, items):
        self.items = list(items)
        self.i = 0

    def next(self):
        it = self.items[self.i % len(self.items)]
        self.i += 1
        return it


def build_program(taps=()):
    nc = bass.Bass("TRN2", target_bir_lowering=False)
    S = Sched(nc)
    taps = set(taps)

    def din(name, shape, dt):
        return nc.dram_tensor(name, shape, dt, kind="ExternalInput").ap()

    def dscr(name, shape, dt):
        kind = "ExternalOutput" if name in taps else "Internal"
        return nc.dram_tensor(name, shape, dt, kind=kind).ap()

    x_d = din("x", [2, SEQ, DM], F32)
    p_d = din("p", [2, 2, SEQ, 256], F32)
    pos_d = din("pos", [2, SEQ], I32)
    w_in_d = din("w_in", [2, DM, 3008], F32)
    w_uq_d = din("w_uq", [2, 256, 768], F32)
    w_ukv_d = din("w_ukv", [2, 128, 1024], F32)
    w_out_d = din("w_out", [2, DM, DM], F32)
    w_pe_d = din("w_pe", [2, 256, DM], F32)
    w_pg_d = din("w_pg", [2, DM, DM], F32)
    gmix_d = din("gmix", [128, 2, 8], F32)
    gple_d = din("gple", [128, 2, 8], F32)
    bpg_d = din("bpg", [128, 2, 8], F32)
    gq_d = din("gq", [128, 2, 2], F32)
    gkv_d = din("gkv", [128, 2, 1], F32)
    cfar_d = din("cfar", [128, 2, 8], F32)
    extin_d = din("extb", [2, 8, 384], F32)
    gfin_d = din("gfin", [128, DM], F32)
    invf_d = din("invf", [128, 1], F32)
    sgn_d = din("sgn", [128, 1], F32)
    y_d = nc.dram_tensor("y", [2, SEQ, DM], F32, kind="ExternalOutput").ap()

    qTa_d = dscr("qTa", [128, NSLAB, 4, SL], BF16)
    kTa_d = dscr("kTa", [128, NSLAB, 4, SL], BF16)
    va_d = dscr("va", [128, NT, 8 * 65], BF16)
    sga_d = dscr("sga", [128, NT, 512], BF16)
    qb_d = dscr("qb", [128, NSLAB, 6, SL], BF16)
    kTb_d = dscr("kTb", [128, NSLAB, 4, SL], BF16)
    kr_d = dscr("kr", [128, SEQ], BF16)
    vb_d = dscr("vb", [128, NT, 4 * 129], BF16)
    sgb_d = dscr("sgb", [128, NT, 512], BF16)
    ropes_d = dscr("ropes", [2, 128, SEQ], F32)
    dbg = {}
    for nm, shp, dt in (("dbg_h0", [128, 8, SEQ], F32), ("dbg_h1", [128, 8, SEQ], F32), ("dbg_h2", [128, 8, SEQ], F32),
                        ("dbg_mixa", [128, 4, SEQ], BF16), ("dbg_mixb", [128, 4, SEQ], BF16),
                        ("dbg_E", [128, 8, 384], BF16), ("dbg_rope", [2, 128, SEQ], F32)):
        if nm in taps:
            dbg[nm] = nc.dram_tensor(nm, shp, dt, kind="ExternalOutput").ap()
    stop_after = None
    for tname in taps:
        if tname.startswith("stop:"):
            stop_after = tname[5:]

    qTa_r = [[Region(f"qTa{c}_{t}") for t in range(NSLAB)] for c in range(4)]
    kTa_r = [[Region(f"kTa{c}_{t}") for t in range(NSLAB)] for c in range(4)]
    va_r = [Region(f"va{i}") for i in range(NT)]
    sga_r = [Region(f"sga{i}") for i in range(NT)]
    qb_r = [[Region(f"qb{c}_{t}") for t in range(NSLAB)] for c in range(6)]
    kTb_r = [[Region(f"kTb{c}_{t}") for t in range(NSLAB)] for c in range(4)]
    kr_r = [Region(f"kr{t}") for t in range(NSLAB)]
    vb_r = [Region(f"vb{i}") for i in range(NT)]
    sgb_r = [Region(f"sgb{i}") for i in range(NT)]

    top = contextlib.ExitStack()

    sbn = [0]

    def SB(stack, name, shape, dt):
        sbn[0] += 1
        return stack.enter_context(nc.sbuf_tensor(f"sb{sbn[0]}_{name}", shape, dt))

    def ACT(out, in_, func, reads, writes, **kw):
        S.op("act", lambda h: h.activation(out=out, in_=in_, func=func, **kw), reads=reads, writes=writes)

    def TT(out, in0, in1, op, reads, writes, eng="dve"):
        S.op(eng, lambda h: h.tensor_tensor(out=out, in0=in0, in1=in1, op=op), reads=reads, writes=writes)

    def TS(out, in0, s1, op0, reads, writes, s2=None, op1=None, eng="dve"):
        if op1 is None:
            S.op(eng, lambda h: h.tensor_scalar(out=out, in0=in0, scalar1=s1, scalar2=None, op0=op0), reads=reads, writes=writes)
        else:
            S.op(eng, lambda h: h.tensor_scalar(out=out, in0=in0, scalar1=s1, scalar2=s2, op0=op0, op1=op1), reads=reads,
                 writes=writes)

    def STT(out, in0, scalar, in1, op0, op1, reads, writes, eng="dve"):
        S.op(eng, lambda h: h.scalar_tensor_tensor(out=out, in0=in0, scalar=scalar, in1=in1, op0=op0, op1=op1), reads=reads,
             writes=writes)

    def COPY(out, in_, reads, writes, eng="dve"):
        S.op(eng, lambda h: h.tensor_copy(out, in_), reads=reads, writes=writes)

    def RECIP(out, in_, reads, writes):
        S.op("dve", lambda h: h.reciprocal(out=out, in_=in_), reads=reads, writes=writes)

    def MEMSET(ap, val, reads, writes, eng="pool"):
        S.op(eng, lambda h: h.memset(ap, val), reads=reads, writes=writes)

    def dma(out, in_, reads, writes, key, eng="sp"):
        S.op(eng, lambda h: h.dma_start(out=out, in_=in_), reads=reads, writes=writes, dma_key=key)

    def mm(out, lhsT, rhs, start, stop, reads, writes, last, skip=False):
        S.op("pe", lambda h: h.matmul(out, lhsT=lhsT, rhs=rhs, start=start, stop=stop, skip_group_check=skip),
             reads=reads, writes=writes, inc=last)

    def tr(out, in_, ident, reads, writes, last):
        S.op("pe", lambda h: h.transpose(out, in_, ident), reads=reads, writes=writes, inc=last)

    def bc_last(ap2d, n):
        return bass.AP(ap2d.tensor, ap2d.offset, [list(ap2d.ap[0]), list(ap2d.ap[1]), [0, n]])

    with top:
        PSA = top.enter_context(nc.psum_tensor("psa", [128, 8 * 512], F32))
        PS = [PSA[:, i * 512:(i + 1) * 512] for i in range(8)]
        PSB = [PSA[:, i * 512:(i + 1) * 512].bitcast(BF16) for i in range(8)]
        PR = [Region(f"ps{i}") for i in range(8)]

        hT = SB(top, "hT", [128, 8, SEQ], F32)
        hT_r = [Region(f"hT{t}") for t in range(NSLAB)]
        ropes_r = Region("ropes_dram")
        sgn = SB(top, "sgn", [128, 1], F32)
        ident_f = SB(top, "ident_f", [128, 128], F32)
        ident_b = SB(top, "ident_b", [128, 128], BF16)
        ones_f = SB(top, "ones_f", [128, 128], F32)
        ones_b = SB(top, "ones_b", [128, 128], BF16)
        epsb = SB(top, "epsb", [128, 1], F32)
        const_r = Region("const")
        gmix = SB(top, "gmix", [128, 2, 8], F32)
        ngmix = SB(top, "ngmix", [128, 2, 8], F32)
        gple = SB(top, "gple", [128, 2, 8], F32)
        hbpg = SB(top, "hbpg", [128, 2, 8], F32)
        gq = SB(top, "gq", [128, 2, 2], F32)
        ngq = SB(top, "ngq", [128, 2, 2], F32)
        gkv = SB(top, "gkv", [128, 2, 1], F32)
        cneg = SB(top, "cneg", [128, 2, 8], F32)
        invf = SB(top, "invf", [128, 1], F32)
        par_r = Region("params")
        rcolP = SB(top, "rcolP", [128, NT], F32)
        rcolP_r = Region("rcolP")

        MEMSET(ident_f[:], 1.0, [], [const_r])
        S.op("pool", lambda h: h.affine_select(out=ident_f[:], in_=ident_f[:], pattern=[[-1, 128]],
                                               compare_op=ALU.is_equal, fill=0.0, base=0, channel_multiplier=1),
             reads=[const_r], writes=[const_r])
        MEMSET(ones_f[:], 1.0, [], [const_r])
        MEMSET(ones_b[:], 1.0, [], [const_r])
        MEMSET(epsb[:], EPS, [], [const_r])
        COPY(ident_b[:], ident_f[:], [const_r], [const_r])
        for (t_sb, t_d) in ((gmix, gmix_d), (gple, gple_d), (hbpg, bpg_d), (gq, gq_d), (gkv, gkv_d), (cneg, cfar_d)):
            dma(t_sb[:], t_d, [], [par_r], "d_par")
        dma(invf[:], invf_d, [], [par_r], "d_par")
        dma(sgn[:], sgn_d, [], [par_r], "d_par")
        TS(ngmix[:], gmix[:], -1.0, ALU.mult, [par_r], [par_r])
        TS(ngq[:], gq[:], -1.0, ALU.mult, [par_r], [par_r])
        TS(hbpg[:], hbpg[:], 0.5, ALU.mult, [par_r], [par_r])
        TS(cneg[:], cneg[:], -1.0, ALU.mult, [par_r], [par_r])
        Etl = [SB(top, f"Et{l}", [128, 8, 384], BF16) for l in range(2)]
        E_r = Region("E")
        S.barrier()

        def norm_cols(sq_ap, sq_reads, nchunk, n, col0, scale, rcol, rcol_r):
            for j in range(n):
                for c in range(nchunk):
                    mm(PS[2][:, col0 + j:col0 + j + 1], sq_ap(c, j), ones_b[:, 0:1], c == 0, c == nchunk - 1,
                       sq_reads + [const_r], [PR[2]], c == nchunk - 1)
            TS(rcol[:, col0:col0 + n], PS[2][:, col0:col0 + n], scale, ALU.mult, [PR[2]], [rcol_r])

        def rsqrt_cols(rc, rc_r):
            ACT(rc, rc, AF.Sqrt, [rc_r, const_r], [rc_r], bias=epsb[:, 0:1], scale=1.0)
            RECIP(rc, rc, [rc_r], [rc_r])

        def bcast_cols(rcol, rcol_r, j0, n, dg, dg_r, out_ap, out_r):
            for j in range(n):
                i = j % len(dg)
                TS(dg[i][:], ident_f[:], rcol[:, j0 + j:j0 + j + 1], ALU.mult, [rcol_r, const_r], [dg_r[i]])
                mm(PS[3][:, j * 128:(j + 1) * 128], ones_f[:], dg[i][:], True, True, [dg_r[i], const_r], [PR[3]], True)
            ACT(out_ap, PS[3][:, 0:n * 128], AF.Copy, [PR[3]], [out_r])

        stopped = False
        for s in range(2):
            if stopped:
                break
            with contextlib.ExitStack() as ph:
                xs = [SB(ph, f"xs{i}", [128, DM], F32) for i in range(4)]
                xs_r = [Region(f"xs{i}") for i in range(4)]
                xsq = SB(ph, "xsq", [128, DM], BF16)
                xsq_r = Region("xsq")
                posi = SB(ph, "posi", [128, SEQ], I32)
                angs = [SB(ph, f"ang{i}", [128, SEQ], F32) for i in range(2)]
                kf = SB(ph, "kf", [128, SEQ], F32)
                ki = SB(ph, "ki", [128, SEQ], I32)
                tmp_r = Region("ropetmp")
                cos4 = SB(ph, "cos4", [128, SEQ], F32)
                sin4 = SB(ph, "sin4", [128, SEQ], F32)
                rope_r = Region("rope")
                dma(posi[:], bass.AP(pos_d.tensor, pos_d[s, 0:1].offset, [[0, 128], [1, SEQ]]), [], [tmp_r], "d_pos")
                if s == 0:
                    tp = [SB(ph, f"tp{i}", [128, 128], F32) for i in range(32)]
                    tpall_r = Region("tpall")
                    for l_ in range(2):
                        for hh in range(8):
                            for bi, off in ((1, 0), (2, 128)):
                                i = l_ * 16 + hh * 2 + bi - 1
                                srcT = bass.AP(extin_d.tensor, extin_d[l_, hh, off:off + 1].offset, [[1, 128], [1, 128]])
                                dma(tp[i][:], srcT, [], [tpall_r], "d_tp", eng="pool")
                T = [tmp_r]
                for (ang, shift) in ((angs[0], 0.0), (angs[1], float(np.pi / 2))):
                    COPY(ang[:], posi[:], T, T)
                    TS(ang[:], ang[:], invf[:, 0:1], ALU.mult, T + [par_r], T)
                    if shift != 0.0:
                        TS(ang[:], ang[:], shift, ALU.add, T, T)
                    TS(kf[:], ang[:], float(1 / (2 * np.pi)), ALU.mult, T, T)
                    COPY(ki[:], kf[:], T, T)
                    COPY(kf[:], ki[:], T, T)
                    STT(ang[:], kf[:], -TWO_PI_HI, ang[:], ALU.mult, ALU.add, T, T)
                    STT(ang[:], kf[:], -TWO_PI_LO, ang[:], ALU.mult, ALU.add, T, T)
                    TS(kf[:], ang[:], float(np.pi), ALU.is_gt, T, T, s2=float(-2 * np.pi), op1=ALU.mult)
                    TT(ang[:], ang[:], kf[:], ALU.add, T, T)
                    TS(kf[:], ang[:], float(-np.pi), ALU.is_lt, T, T, s2=float(2 * np.pi), op1=ALU.mult)
                    TT(ang[:], ang[:], kf[:], ALU.add, T, T)
                prot = Rot([0, 1, 2, 3])
                for i in range(NT):
                    b = i % 4
                    dma(xs[b][:], x_d[s, i * 128:(i + 1) * 128, :], [], [xs_r[b]], f"d_xs{b}")
                    ACT(xsq[:], xs[b][:], AF.Square, [xs_r[b]], [xsq_r, rcolP_r], accum_out=rcolP[:, i:i + 1])
                    bk0 = prot.next()
                    bk1 = prot.next()
                    for g, bk in ((0, bk0), (1, bk1)):
                        for c4 in range(4):
                            c = g * 4 + c4
                            tr(PS[bk][:, c4 * 128:(c4 + 1) * 128], xs[b][:, c * 128:(c + 1) * 128], ident_f[:],
                               [xs_r[b], const_r], [PR[bk]], c4 == 3)
                    ACT(hT[:, :, i * 128:(i + 1) * 128], PSA[:, bk0 * 512:bk0 * 512 + 1024].rearrange("p (c t) -> p c t", t=128),
                        AF.Copy, [PR[bk0], PR[bk1]], [hT_r[i // 4]])
                if s == 0:
                    for l_ in range(2):
                        for hh in range(8):
                            MEMSET(Etl[l_][:, hh, 0:128], 1.0, [], [E_r])
                            for bi in (1, 2):
                                tv = tp[l_ * 16 + hh * 2 + bi - 1][:]
                                rev = bass.AP(tv.tensor, tv.offset + 127, [list(tv.ap[0]), [-1, 128]])
                                ACT(Etl[l_][:, hh, bi * 128:(bi + 1) * 128], rev, AF.Exp, [tpall_r, par_r], [E_r],
                                    bias=cneg[:, l_, hh:hh + 1], scale=1.0)
                    for l_ in range(2):
                        for hh in range(8):
                            MEMSET(Etl[l_][0:64, hh, 64:128], 0.0, [E_r], [E_r])
                            MEMSET(Etl[l_][64:128, hh, 256:320], 0.0, [E_r], [E_r])
                ACT(sin4[:], angs[0][:], AF.Sin, T, [rope_r])
                ACT(cos4[:], angs[1][:], AF.Sin, T, [rope_r])
                TS(sin4[:], sin4[:], sgn[:, 0:1], ALU.mult, [rope_r, par_r], [rope_r])
                dma(ropes_d[0], cos4[:], [rope_r], [ropes_r], "d_ropes")
                dma(ropes_d[1], sin4[:], [rope_r], [ropes_r], "d_ropes")
                if "dbg_rope" in dbg and s == 0:
                    dma(dbg["dbg_rope"][0], cos4[:], [rope_r], [Region("d")], "d_dbg")
                    dma(dbg["dbg_rope"][1], sin4[:], [rope_r], [Region("d")], "d_dbg")
                if "dbg_h0" in dbg and s == 0:
                    dma(dbg["dbg_h0"], hT[:, :, :], hT_r, [Region("d")], "d_dbg")
                S.barrier()
            if stop_after == "load":
                stopped = True
                break

            for l in range(2):
                with contextlib.ExitStack() as ph:
                    hbf = SB(ph, "hbf", [128, 8, SEQ], BF16)
                    hbf_r = [Region(f"hbf{t}") for t in range(NSLAB)]
                    sq = SB(ph, "sq", [128, 8, SL], BF16)
                    sq_r = Region("sq")
                    rcol = SB(ph, "rcol", [128, NT], F32)
                    rcol_r = Region("rcol")
                    rbc = SB(ph, "rbc", [128, SEQ], F32)
                    rbc_r = [Region(f"rbc{t}") for t in range(NSLAB)]
                    dg = [SB(ph, f"dg{i}", [128, 128], F32) for i in range(2)]
                    dg_r = [Region(f"dg{i}") for i in range(2)]
                    cos4 = SB(ph, "cos4", [128, SEQ], F32)
                    sin4 = SB(ph, "sin4", [128, SEQ], F32)
                    rope_r = Region("rope")
                    dma(cos4[:], ropes_d[0], [ropes_r], [rope_r], "d_ropel")
                    dma(sin4[:], ropes_d[1], [ropes_r], [rope_r], "d_ropel")
                    wg = [SB(ph, f"wg{i}", [128, 8, 640], BF16) for i in range(2)]
                    wg_r = [Region(f"wg{i}") for i in range(2)]
                    wq = SB(ph, "wq", [128, 2, 1024], BF16)
                    wkv = SB(ph, "wkv", [128, 1024], BF16)
                    wq_r = Region("wq")
                    ost = [SB(ph, f"ost{i}", [128, SL], BF16) for i in range(6)]
                    ost_r = [Region(f"ost{i}") for i in range(6)]
                    vaug = [SB(ph, f"vaug{i}", [128, 8, 65], BF16) for i in range(3)]
                    vaug_r = [Region(f"vaug{i}") for i in range(3)]
                    vbaug = [SB(ph, f"vbaug{i}", [128, 4, 129], BF16) for i in range(3)]
                    vbaug_r = [Region(f"vbaug{i}") for i in range(3)]
                    f32a = [SB(ph, f"f32a{i}", [128, SL], F32) for i in range(4)]
                    f32a_r = [Region(f"f32a{i}") for i in range(4)]
                    cq32 = SB(ph, "cq32", [128, 2, SL], F32)
                    ckv32 = SB(ph, "ckv32", [128, SL], F32)
                    c32_r = Region("c32")
                    sqq = SB(ph, "sqq", [128, 3, SL], BF16)
                    sqq_r = Region("sqq")
                    rqc = SB(ph, "rqc", [128, 8], F32)
                    rqc_r = Region("rqc")
                    rqb = SB(ph, "rqb", [128, 2, SL], F32)
                    rqb_r = Region("rqb")
                    cqn = SB(ph, "cqn", [128, 2, SL], BF16)
                    ckvn = SB(ph, "ckvn", [128, SL], BF16)
                    cn_r = Region("cn")

                    for i in range(3):
                        MEMSET(vaug[i][:], 1.0, [], [vaug_r[i]])
                        MEMSET(vbaug[i][:], 1.0, [], [vbaug_r[i]])

                    w_in3 = w_in_d[l].rearrange("(c p) n -> p c n", p=128)

                    def load_group(slot, c0, ncol, special=False):
                        key = f"d_wg{slot}"
                        if not special:
                            dma(wg[slot][:, :, 0:ncol], w_in3[:, :, c0:c0 + ncol], [], [wg_r[slot]], key, eng="pool")
                        else:
                            dma(wg[slot][:, :, 0:448], w_in3[:, :, c0:c0 + 448], [], [wg_r[slot]], key, eng="pool")
                            dma(wg[slot][:, :, 448:512], w_in3[:, :, c0 + 384:c0 + 448], [], [wg_r[slot]], key, eng="pool")
                            for d0 in (512, 576):
                                dma(wg[slot][:, :, d0:d0 + 32], w_in3[:, :, c0 + 416:c0 + 448], [], [wg_r[slot]], key, eng="pool")
                                dma(wg[slot][:, :, d0 + 32:d0 + 64], w_in3[:, :, c0 + 384:c0 + 416], [], [wg_r[slot]], key, eng="pool")

                    load_group(0, 0, 512)
                    load_group(1, 512, 512)
                    wuq3 = w_uq_d[l].rearrange("(kc p) (h d) -> p kc h d", p=128, d=192)
                    for kc in range(2):
                        dma(wq[:, kc, 0:512].rearrange("p (h d) -> p h d", d=128), wuq3[:, kc, :, 0:128], [], [wq_r], "d_wq",
                            eng="pool")
                    for kc in range(2):
                        for hh in range(4):
                            pp, half = hh % 2, hh // 2
                            ra = 512 + pp * 128 + half * 64
                            rb_ = 768 + pp * 128 + half * 64
                            dma(wq[:, kc, ra:ra + 64], wuq3[:, kc, hh, 128:192], [], [wq_r], "d_wq", eng="pool")
                            dma(wq[:, kc, rb_:rb_ + 32], wuq3[:, kc, hh, 160:192], [], [wq_r], "d_wq", eng="pool")
                            dma(wq[:, kc, rb_ + 32:rb_ + 64], wuq3[:, kc, hh, 128:160], [], [wq_r], "d_wq", eng="pool")
                    wkv4 = w_ukv_d[l].rearrange("p (h t d) -> p h t d", t=2, d=128)
                    dma(wkv[:, 0:512].rearrange("p (h d) -> p h d", d=128), wkv4[:, :, 0, :], [], [wq_r], "d_wq", eng="pool")
                    dma(wkv[:, 512:1024].rearrange("p (h d) -> p h d", d=128), wkv4[:, :, 1, :], [], [wq_r], "d_wq", eng="pool")


                    if l == 0:
                        TS(rcol[:, :], rcolP[:, :], 1.0 / DM, ALU.mult, [rcolP_r], [rcol_r])
                    for t in range(NSLAB):
                        sl = slice(t * SL, (t + 1) * SL)
                        TT(hbf[:, :, sl], hT[:, :, sl], bc_last(gmix[:, l, :], SL), ALU.mult, [hT_r[t], par_r], [hbf_r[t]])
                        if l > 0:
                            ACT(sq[:, :, :], hT[:, :, sl], AF.Square, [hT_r[t]], [sq_r])
                            norm_cols(lambda c, j: sq[:, c, j * 128:(j + 1) * 128], [sq_r], 8, 4, t * 4, 1.0 / DM, rcol, rcol_r)

                    def z_mm(slot, mc, t, bk):
                        for c in range(8):
                            mm(PS[bk], wg[slot][:, c, mc * 128:(mc + 1) * 128], hbf[:, c, t * SL:(t + 1) * SL],
                               c == 0, c == 7, [wg_r[slot], hbf_r[t]], [PR[bk]], c == 7)

                    pre_z = []
                    for (bk, mc, t) in ((0, 0, 0), (1, 0, 1), (4, 0, 2), (5, 0, 3), (6, 1, 0), (7, 1, 1)):
                        z_mm(0, mc, t, bk)
                        pre_z.append((bk, mc, t))
                    rsqrt_cols(rcol[:, :], rcol_r)
                    for t in range(NSLAB):
                        bcast_cols(rcol, rcol_r, t * 4, 4, dg, dg_r, rbc[:, t * SL:(t + 1) * SL], rbc_r[t])

                    zrot = Rot([0, 1, 4, 5])
                    orot = Rot(range(6))

                    def fm_group(slot, nmc, dst_d, dst_r, pre=()):
                        pre = list(pre)
                        for mc in range(nmc):
                            for t in range(NSLAB):
                                if pre and pre[0][1:] == (mc, t):
                                    bk = pre.pop(0)[0]
                                else:
                                    bk = zrot.next()
                                    z_mm(slot, mc, t, bk)
                                o = orot.next()
                                TT(ost[o][:], PS[bk], rbc[:, t * SL:(t + 1) * SL], ALU.mult, [PR[bk], rbc_r[t]], [ost_r[o]])
                                dma(dst_d[:, t, mc, :], ost[o][:], [ost_r[o]], [dst_r[mc][t]], f"d_ost{o}")

                    def gate_group(slot, dst_d, dst_r):
                        pendg = None

                        def fin(i, fu, ft):
                            o = orot.next()
                            STT(ost[o][:], f32a[ft][:], 1.0, f32a[fu][:], ALU.add, ALU.mult, [f32a_r[fu], f32a_r[ft]], [ost_r[o]])
                            dma(dst_d[:, i, :], ost[o][:], [ost_r[o]], [dst_r[i]], f"d_ost{o}")

                        for i in range(NT):
                            bk = zrot.next()
                            for c in range(8):
                                mm(PS[bk], hbf[:, c, i * 128:(i + 1) * 128], wg[slot][:, c, 0:512], c == 0, c == 7,
                                   [wg_r[slot], hbf_r[i // 4]], [PR[bk]], c == 7)
                            fu, ft = (0, 1) if i % 2 == 0 else (2, 3)
                            TS(f32a[fu][:], PS[bk], rcol[:, i:i + 1], ALU.mult, [PR[bk], rcol_r], [f32a_r[fu]])
                            ACT(f32a[ft][:], f32a[fu][:], AF.Tanh, [f32a_r[fu]], [f32a_r[ft]], scale=0.5)
                            if pendg is not None:
                                fin(*pendg)
                            pendg = (i, fu, ft)
                        fin(*pendg)

                    fm_group(0, 4, qTa_d, qTa_r, pre_z)
                    load_group(0, 1024, 512)
                    fm_group(1, 4, kTa_d, kTa_r)
                    load_group(1, 1536, 512)
                    vrot = Rot(range(3))
                    for i in range(NT):
                        bk = zrot.next()
                        for c in range(8):
                            mm(PS[bk], hbf[:, c, i * 128:(i + 1) * 128], wg[0][:, c, 0:512], c == 0, c == 7,
                               [wg_r[0], hbf_r[i // 4]], [PR[bk]], c == 7)
                        v = vrot.next()
                        ACT(vaug[v][:, :, 0:64], PS[bk].rearrange("p (h d) -> p h d", d=64), AF.Copy, [PR[bk], rcol_r],
                            [vaug_r[v]], scale=rcol[:, i:i + 1])
                        dma(va_d[:, i, :], vaug[v][:, :, :].rearrange("p h d -> p (h d)"), [vaug_r[v]],
                            [va_r[i]], f"d_vaug{v}")
                    load_group(0, 2048, 448, special=True)
                    gate_group(1, sga_d, sga_r)
                    load_group(1, 2496, 512)
                    urot = Rot([0, 1, 4, 5, 6, 7])
                    vbrot = Rot(range(3))
                    def G4Z(t):
                        sl = slice(t * SL, (t + 1) * SL)
                        for mc in range(5):
                            bk = zrot.next() if mc < 3 else (6 if mc == 3 else 7)
                            for c in range(8):
                                mm(PS[bk], wg[0][:, c, mc * 128:(mc + 1) * 128], hbf[:, c, sl], c == 0, c == 7,
                                   [wg_r[0], hbf_r[t]], [PR[bk]], c == 7)
                            if mc < 2:
                                TT(cq32[:, mc, :], PS[bk], rbc[:, sl], ALU.mult, [PR[bk], rbc_r[t]], [c32_r])
                            elif mc == 2:
                                TT(ckv32[:, :], PS[bk], rbc[:, sl], ALU.mult, [PR[bk], rbc_r[t]], [c32_r])
                        TT(f32a[0][:], PS[6], cos4[:, sl], ALU.mult, [PR[6], rope_r], [f32a_r[0]])
                        TT(f32a[1][:], PS[7], sin4[:, sl], ALU.mult, [PR[7], rope_r], [f32a_r[1]])
                        TT(f32a[0][:], f32a[0][:], f32a[1][:], ALU.add, [f32a_r[0], f32a_r[1]], [f32a_r[0]])
                        o = orot.next()
                        TT(ost[o][:], f32a[0][:], rbc[:, sl], ALU.mult, [f32a_r[0], rbc_r[t]], [ost_r[o]])
                        dma(kr_d[:, sl], ost[o][:], [ost_r[o]], [kr_r[t]], f"d_ost{o}")

                    def G4CH(t):
                        sl = slice(t * SL, (t + 1) * SL)
                        ACT(sqq[:, 0:2, :], cq32[:, :, :], AF.Square, [c32_r], [sqq_r])
                        ACT(sqq[:, 2, :], ckv32[:, :], AF.Square, [c32_r], [sqq_r])
                        norm_cols(lambda c, j: sqq[:, c, j * 128:(j + 1) * 128], [sqq_r], 2, 4, 0, 1.0 / 256, rqc, rqc_r)
                        norm_cols(lambda c, j: sqq[:, 2, j * 128:(j + 1) * 128], [sqq_r], 1, 4, 4, 1.0 / 128, rqc, rqc_r)

                    def G4CH2(t):
                        rsqrt_cols(rqc[:, :], rqc_r)
                        bcast_cols(rqc, rqc_r, 0, 4, dg, dg_r, rqb[:, 0, :], rqb_r)
                        bcast_cols(rqc, rqc_r, 4, 4, dg, dg_r, rqb[:, 1, :], rqb_r)

                    def G4CH3(t):
                        for kc in range(2):
                            STT(cqn[:, kc, :], cq32[:, kc, :], gq[:, l, kc:kc + 1], rqb[:, 0, :], ALU.mult, ALU.mult,
                                [c32_r, rqb_r, par_r], [cn_r])
                        STT(ckvn[:, :], ckv32[:, :], gkv[:, l, 0:1], rqb[:, 1, :], ALU.mult, ALU.mult, [c32_r, rqb_r, par_r], [cn_r])

                    def G4UP(t):
                        sl = slice(t * SL, (t + 1) * SL)
                        for hh in range(4):
                            bk = urot.next()
                            for kc in range(2):
                                mm(PS[bk], wq[:, kc, hh * 128:(hh + 1) * 128], cqn[:, kc, :], kc == 0, kc == 1, [wq_r, cn_r],
                                   [PR[bk]], kc == 1)
                            o = orot.next()
                            ACT(ost[o][:], PS[bk], AF.Copy, [PR[bk]], [ost_r[o]])
                            dma(qb_d[:, t, hh, :], ost[o][:], [ost_r[o]], [qb_r[hh][t]], f"d_ost{o}")
                        for pp in range(2):
                            bA, bB = urot.next(), urot.next()
                            fa, fb = (0, 1) if pp == 0 else (2, 3)
                            for kc in range(2):
                                mm(PS[bA], wq[:, kc, 512 + pp * 128:640 + pp * 128], cqn[:, kc, :], kc == 0, kc == 1, [wq_r, cn_r],
                                   [PR[bA]], kc == 1)
                            for kc in range(2):
                                mm(PS[bB], wq[:, kc, 768 + pp * 128:896 + pp * 128], cqn[:, kc, :], kc == 0, kc == 1, [wq_r, cn_r],
                                   [PR[bB]], kc == 1)
                            TT(f32a[fa][:], PS[bA], cos4[:, sl], ALU.mult, [PR[bA], rope_r], [f32a_r[fa]])
                            TT(f32a[fb][:], PS[bB], sin4[:, sl], ALU.mult, [PR[bB], rope_r], [f32a_r[fb]])
                            o = orot.next()
                            TT(ost[o][:], f32a[fa][:], f32a[fb][:], ALU.add, [f32a_r[fa], f32a_r[fb]], [ost_r[o]])
                            dma(qb_d[:, t, 4 + pp, :], ost[o][:], [ost_r[o]], [qb_r[4 + pp][t]], f"d_ost{o}")

                    def G4UPkv(t):
                        sl = slice(t * SL, (t + 1) * SL)
                        for hh in range(4):
                            bk = urot.next()
                            mm(PS[bk], wkv[:, hh * 128:(hh + 1) * 128], ckvn[:, :], True, True, [wq_r, cn_r], [PR[bk]], True)
                            o = orot.next()
                            COPY(ost[o][:], PS[bk], [PR[bk]], [ost_r[o]])
                            dma(kTb_d[:, t, hh, :], ost[o][:], [ost_r[o]], [kTb_r[hh][t]], f"d_ost{o}")
                        for j in range(4):
                            i = t * 4 + j
                            bk = urot.next()
                            mm(PS[bk], ckvn[:, j * 128:(j + 1) * 128], wkv[:, 512:1024], True, True, [wq_r, cn_r], [PR[bk]], True)
                            v = vbrot.next()
                            ACT(vbaug[v][:, :, 0:128], PS[bk].rearrange("p (h d) -> p h d", d=128), AF.Copy, [PR[bk]],
                                [vbaug_r[v]])
                            dma(vb_d[:, i, :], vbaug[v][:, :, :].rearrange("p h d -> p (h d)"), [vbaug_r[v]],
                                [vb_r[i]], f"d_vbaug{v}")

                    G4Z(0)
                    G4CH(0)
                    G4CH2(0)
                    G4CH3(0)
                    for t in range(NSLAB):
                        nxt = t + 1 < NSLAB
                        if nxt:
                            G4Z(t + 1)
                            G4CH(t + 1)
                        G4UP(t)
                        if nxt:
                            G4CH2(t + 1)
                        G4UPkv(t)
                        if nxt:
                            G4CH3(t + 1)
                    gate_group(1, sgb_d, sgb_r)
                    S.barrier()
                if stop_after == "p1":
                    stopped = True
                    break

                with contextlib.ExitStack() as ph23:
                    mixTa = SB(ph23, "mixTa", [128, 4, SEQ], BF16)
                    mixTb = SB(ph23, "mixTb", [128, 4, SEQ], BF16)
                    mixT_r = [Region(f"mixT{i}") for i in range(NT)]
                    with contextlib.ExitStack() as ph:
                        kT = SB(ph, "kT", [128, 4, SEQ], BF16)
                        kAB_r = [Region(f"kAB{t}") for t in range(NSLAB)]
                        vres = SB(ph, "vres", [128, NT, 8 * 65], BF16)
                        vres_r = [Region(f"vres{t}") for t in range(NSLAB)]
                        qtA = [SB(ph, f"qtA{i}", [128, 4, 128], BF16) for i in range(2)]
                        qtB = [SB(ph, f"qtB{i}", [128, 4, 128], BF16) for i in range(2)]
                        qt_r = [Region(f"qt{i}") for i in range(2)]
                        sgt = [SB(ph, f"sgt{i}", [128, 512], BF16) for i in range(2)]
                        sgt_r = [Region(f"sgt{i}") for i in range(2)]
                        PT = [SB(ph, f"PT{i}", [128, 640], BF16) for i in range(4)]
                        PT_r = [Region(f"PT{i}") for i in range(4)]
                        rcp = SB(ph, "rcp", [128, 8], F32)
                        rcp_r = Region("rcp")
                        on = SB(ph, "on", [128, 512], F32)
                        on_r = Region("on")
                        mx = [SB(ph, f"mx{i}", [128, 512], BF16) for i in range(2)]
                        mx_r = [Region(f"mx{i}") for i in range(2)]
                        def load_K(t):
                            sl = slice(t * SL, (t + 1) * SL)
                            dma(kT[:, :, sl], kTa_d[:, t, :, :], [kTa_r[c][t] for c in range(4)], [kAB_r[t]], f"d_kAB{t}")

                        def load_V(t):
                            dma(vres[:, t * 4:(t + 1) * 4, :], va_d[:, t * 4:(t + 1) * 4, :], va_r[t * 4:(t + 1) * 4], [vres_r[t]],
                                f"d_vres{t}")

                        def load_slab(t):
                            load_K(t)
                            load_V(t)

                        qtz_r = [Region(f"qtz{i}") for i in range(2)]
                        for i in range(2):
                            MEMSET(qtA[i][64:128, :, :], 0.0, [], [qtz_r[i]])
                            MEMSET(qtB[i][0:64, :, :], 0.0, [], [qtz_r[i]])
                        load_K(0)
                        Et = Etl[l]
                        SC = [0, 2, 4]
                        slotcol = {1: 0, 2: 128, 0: 256, 3: 384, 4: 512}
                        ptrot = Rot(range(4))
                        prev_tail = None
                        for pr in range(NT):
                            qb_ = pr % 2
                            qsrc = [qTa_r[c][pr // 4] for c in range(4)]
                            pc0 = (pr % 4) * 128
                            dma(qtA[qb_][0:64, :, :], qTa_d[0:64, pr // 4, :, pc0:pc0 + 128], qsrc, [qt_r[qb_]], f"d_qt{qb_}")
                            dma(qtB[qb_][64:128, :, :], qTa_d[64:128, pr // 4, :, pc0:pc0 + 128], qsrc, [qt_r[qb_]], f"d_qt{qb_}")
                            dma(sgt[qb_][:, :], sga_d[:, pr, :], [sga_r[pr]], [sgt_r[qb_]], f"d_sgt{qb_}")
                            blocks = [b for b in range(5) if pr - 4 + b >= 0]
                            if pr == 0:
                                load_V(0)
                                load_slab(1)
                            if pr % 4 == 1 and pr // 4 + 2 < NSLAB:
                                load_slab(pr // 4 + 2)

                            def scores(hh, sci):
                                c = hh // 2
                                qq = qtA if hh % 2 == 0 else qtB
                                b0 = SC[sci]
                                for b in blocks:
                                    j = pr - 4 + b
                                    col = b0 * 512 + slotcol[b]
                                    mm(PSA[:, col:col + 128], kT[:, c, j * 128:(j + 1) * 128], qq[qb_][:, c, :], True, True,
                                       [kAB_r[j // 4], qt_r[qb_], qtz_r[qb_]], [PR[b0], PR[b0 + 1]], b == blocks[-1])

                            def softmax_pv(hh, sci):
                                b0 = SC[sci]
                                pi = ptrot.next()
                                cols = sorted(slotcol[b] for b in blocks)
                                runs = []
                                for cc in cols:
                                    if runs and runs[-1][1] == cc:
                                        runs[-1][1] = cc + 128
                                    else:
                                        runs.append([cc, cc + 128])
                                for (c0, c1) in runs:
                                    ACT(PT[pi][:, c0:c1], PSA[:, b0 * 512 + c0:b0 * 512 + c1], AF.Exp, [PR[b0], PR[b0 + 1]],
                                        [PT_r[pi]], scale=SCALE_A)
                                if 0 in blocks:
                                    w0, e0 = 256, 0
                                elif 3 in blocks:
                                    w0, e0 = 384, 128
                                else:
                                    w0, e0 = 512, 256
                                TT(PT[pi][:, w0:640], PT[pi][:, w0:640], Et[:, hh, e0:384], ALU.mult, [PT_r[pi], E_r], [PT_r[pi]])
                                ob = 6 + hh // 4
                                oc = (hh % 4) * 65
                                for bi, b in enumerate(blocks):
                                    j = pr - 4 + b
                                    col = slotcol[b]
                                    mm(PS[ob][:, oc:oc + 65], PT[pi][:, col:col + 128], vres[:, j, hh * 65:(hh + 1) * 65],
                                       bi == 0, bi == len(blocks) - 1, [PT_r[pi], vres_r[j // 4]], [PR[ob]], bi == len(blocks) - 1)

                            pend = []
                            for hh in range(8):
                                scores(hh, hh % 3)
                                pend.append((hh, hh % 3))
                                if len(pend) > 2:
                                    softmax_pv(*pend.pop(0))
                            while pend:
                                softmax_pv(*pend.pop(0))
                            if prev_tail is not None:
                                prev_tail()
                            for g in range(2):
                                ov = PS[6 + g][:, 0:260].rearrange("p (h d) -> p h d", d=65)
                                RECIP(rcp[:, g * 4:(g + 1) * 4], ov[:, :, 64], [PR[6 + g]], [rcp_r])
                                TT(on[:, g * 256:(g + 1) * 256].rearrange("p (h d) -> p h d", d=64), ov[:, :, 0:64],
                                   bc_last(rcp[:, g * 4:(g + 1) * 4], 64), ALU.mult, [PR[6 + g], rcp_r], [on_r])
                            mxi = pr % 2
                            STT(mx[mxi][:, :], on[:, :], 0.5, sgt[qb_][:, :], ALU.mult, ALU.mult, [on_r, sgt_r[qb_]], [mx_r[mxi]])

                            def tail(pr=pr, mxi=mxi):
                                for c in range(4):
                                    tr(PSB[5][:, c * 128:(c + 1) * 128], mx[mxi][:, c * 128:(c + 1) * 128], ident_b[:],
                                       [mx_r[mxi], const_r], [PR[5]], c == 3)
                                ACT(mixTa[:, :, pr * 128:(pr + 1) * 128], PSB[5][:, 0:512].rearrange("p (c t) -> p c t", t=128),
                                    AF.Copy, [PR[5]], [mixT_r[pr]])
                            prev_tail = tail
                        if prev_tail is not None:
                            prev_tail()
                        if "dbg_mixa" in dbg and s == 0 and l == 0:
                            dma(dbg["dbg_mixa"], mixTa[:, :, :], mixT_r, [Region("d")], "d_dbg")
                        S.barrier()
                    if stop_after == "p2":
                        stopped = True
                        break
                    wo = SB(ph23, "wo", [128, 8, DM], BF16)
                    wpe = SB(ph23, "wpe", [128, 2, DM], BF16)
                    w4_r = Region("w4")

                    with contextlib.ExitStack() as ph:
                        kn = SB(ph, "kn", [128, 4, SEQ], BF16)
                        kn_r = [Region(f"kn{t}") for t in range(NSLAB)]
                        krT = SB(ph, "krT", [128, SEQ], BF16)
                        vbres = SB(ph, "vbres", [128, NT, 4 * 129], BF16)
                        vbres_r = [Region(f"vbres{t}") for t in range(NSLAB)]
                        qn = [SB(ph, f"qn{i}", [128, 8, SL], BF16) for i in range(2)]
                        qn_r = [Region(f"qn{i}") for i in range(2)]
                        sgb = [SB(ph, f"sgb{i}", [128, 4, 512], BF16) for i in range(2)]
                        sgb_sr = [Region(f"sgbs{i}") for i in range(2)]
                        PTb = [SB(ph, f"PTb{i}", [128, SL], BF16) for i in range(6)]
                        PTb_r = [Region(f"PTb{i}") for i in range(6)]
                        rcpb = SB(ph, "rcpb", [128, 4], F32)
                        rcpb_r = Region("rcpb")
                        onb = SB(ph, "onb", [128, 4, 128], F32)
                        onb_r = Region("onb")
                        mxb = [[SB(ph, f"mxb{k}_{i}", [128, 512], BF16) for i in range(4)] for k in range(2)]
                        mxb_r = [[Region(f"mxb{k}_{i}") for i in range(4)] for k in range(2)]
                        prev_tail_b = None
                        def load_kn_b(t):
                            sl = slice(t * SL, (t + 1) * SL)
                            dma(kn[:, :, sl], kTb_d[:, t, :, :], [kTb_r[c][t] for c in range(4)], [kn_r[t]], f"d_kn{t}")
                            dma(krT[:, sl], kr_d[:, sl], [kr_r[t]], [kn_r[t]], f"d_kn{t}")

                        def load_vb_b(t):
                            dma(vbres[:, t * 4:(t + 1) * 4, :], vb_d[:, t * 4:(t + 1) * 4, :], vb_r[t * 4:(t + 1) * 4], [vbres_r[t]],
                                f"d_vbres{t}")

                        def load_slab_b(t):
                            load_kn_b(t)
                            load_vb_b(t)
                        ptrot = Rot(range(6))
                        scrot = Rot([0, 1, 6, 7])
                        qz_r = [Region(f"qz{i}") for i in range(2)]
                        for i in range(2):
                            for pp in range(2):
                                MEMSET(qn[i][64:128, 4 + 2 * pp, :], 0.0, [], [qz_r[i]])
                                MEMSET(qn[i][0:64, 5 + 2 * pp, :], 0.0, [], [qz_r[i]])
                        def load_qpart(t):
                            qi = t % 2
                            qsrc = [qb_r[c][t] for c in range(6)]
                            dma(qn[qi][:, 0:4, :], qb_d[:, t, 0:4, :], qsrc, [qn_r[qi]], f"d_qn{qi}")
                            for pp in range(2):
                                dma(qn[qi][0:64, 4 + 2 * pp, :], qb_d[0:64, t, 4 + pp, :], qsrc, [qn_r[qi]], f"d_qn{qi}")
                                dma(qn[qi][64:128, 5 + 2 * pp, :], qb_d[64:128, t, 4 + pp, :], qsrc, [qn_r[qi]], f"d_qn{qi}")

                        def load_gate(t):
                            qi = t % 2
                            dma(sgb[qi][:, :, :], sgb_d[:, t * 4:(t + 1) * 4, :], [sgb_r[t * 4 + j] for j in range(4)], [sgb_sr[qi]],
                                f"d_sgb{qi}")

                        load_kn_b(0)
                        load_qpart(0)
                        load_vb_b(0)
                        load_qpart(1)
                        load_gate(0)
                        load_slab_b(1)
                        load_gate(1)
                        for t in range(NSLAB):
                            qi_ = t % 2
                            if t + 2 < NSLAB:
                                load_slab_b(t + 2)
                            if t >= 1 and t + 1 < NSLAB:
                                load_qpart(t + 1)
                                load_gate(t + 1)
                            if t == 0:
                                dma(wo[:, :, :], w_out_d[l].rearrange("(c p) n -> p c n", p=128), [], [w4_r], "d_w4", eng="pool")
                                dma(wpe[:, :, :], w_pe_d[l].rearrange("(c p) n -> p c n", p=128), [], [w4_r], "d_w4", eng="pool")
                            nkb = 4 * t + 4
                            for hp in range(2):
                                heads = (2 * hp, 2 * hp + 1)
                                obks = {hh: ((2, 3) if hh % 2 == 0 else (4, 5)) for hh in heads}
                                first_in_bank = {hh: {obks[hh][0]: True, obks[hh][1]: True} for hh in heads}

                                def scores_b(hh, j):
                                    pp, half = hh % 2, hh // 2
                                    r = max(0, j - 4 * t)
                                    c0 = r * 128
                                    bk = scrot.next()
                                    mm(PS[bk][:, c0:SL], kn[:, hh, j * 128:(j + 1) * 128], qn[qi_][:, hh, c0:SL], True, False,
                                       [kn_r[j // 4], qn_r[qi_]], [PR[bk]], False)
                                    mm(PS[bk][:, c0:SL], krT[:, j * 128:(j + 1) * 128], qn[qi_][:, 4 + 2 * pp + half, c0:SL], False, True,
                                       [kn_r[j // 4], qn_r[qi_], qz_r[qi_]], [PR[bk]], True)
                                    return (hh, j, bk, c0)

                                def pv_b(hh, j, bk, c0):
                                    pi = ptrot.next()
                                    ACT(PTb[pi][:, c0:SL], PS[bk][:, c0:SL], AF.Exp, [PR[bk]], [PTb_r[pi]], scale=SCALE_B)
                                    if j >= 4 * t:
                                        MEMSET(PTb[pi][64:128, c0:c0 + 64], 0.0, [PTb_r[pi]], [PTb_r[pi]])
                                    for qi in range(c0 // 128, 4):
                                        ob = obks[hh][qi // 2]
                                        oc = (qi % 2) * 129
                                        st = first_in_bank[hh][ob]
                                        first_in_bank[hh][ob] = False
                                        mm(PS[ob][:, oc:oc + 129], PTb[pi][:, qi * 128:(qi + 1) * 128],
                                           vbres[:, j, hh * 129:(hh + 1) * 129], st, j == 4 * t + qi, [PTb_r[pi], vbres_r[j // 4]],
                                           [PR[ob]], qi == 3, skip=True)

                                pend = {hh: None for hh in heads}
                                for j in range(nkb):
                                    cur = {hh: scores_b(hh, j) for hh in heads}
                                    for hh in heads:
                                        if pend[hh] is not None:
                                            pv_b(*pend[hh])
                                        pend[hh] = cur[hh]
                                for hh in heads:
                                    pv_b(*pend[hh])
                                if hp == 0 and prev_tail_b is not None:
                                    prev_tail_b()
                                    prev_tail_b = None
                                for hh in heads:
                                    obk = obks[hh]
                                    for g in range(2):
                                        ov = PS[obk[g]][:, 0:258].rearrange("p (q d) -> p q d", d=129)
                                        RECIP(rcpb[:, g * 2:(g + 1) * 2], ov[:, :, 128], [PR[obk[g]]], [rcpb_r])
                                        TT(onb[:, g * 2:(g + 1) * 2, :], ov[:, :, 0:128], bc_last(rcpb[:, g * 2:(g + 1) * 2], 128),
                                           ALU.mult, [PR[obk[g]], rcpb_r], [onb_r])
                                    for qi in range(4):
                                        STT(mxb[t % 2][qi][:, hh * 128:(hh + 1) * 128], onb[:, qi, :], 0.5,
                                            sgb[qi_][:, qi, hh * 128:(hh + 1) * 128], ALU.mult, ALU.mult, [onb_r, sgb_sr[qi_]],
                                            [mxb_r[t % 2][qi]])

                            def tail_b(t=t):
                                for qi in range(4):
                                    i = t * 4 + qi
                                    bq = 6 + (qi % 2)
                                    for c in range(4):
                                        tr(PSB[bq][:, c * 128:(c + 1) * 128], mxb[t % 2][qi][:, c * 128:(c + 1) * 128], ident_b[:],
                                           [mxb_r[t % 2][qi], const_r], [PR[bq]], c == 3)
                                    ACT(mixTb[:, :, i * 128:(i + 1) * 128], PSB[bq][:, 0:512].rearrange("p (c t) -> p c t", t=128),
                                        AF.Copy, [PR[bq]], [mixT_r[i]])
                            prev_tail_b = tail_b
                        if prev_tail_b is not None:
                            prev_tail_b()
                        if "dbg_mixb" in dbg and s == 0 and l == 0:
                            dma(dbg["dbg_mixb"], mixTb[:, :, :], mixT_r, [Region("d")], "d_dbg")
                        fence34 = S.fence()
                    if stop_after == "p3":
                        stopped = True
                        break

                    with contextlib.ExitStack() as ph:
                        wpg = SB(ph, "wpg", [128, 8, DM], BF16)
                        wpg_r = Region("wpg", fence34)
                        dma(wpg[:, :, :], w_pg_d[l].rearrange("(c p) n -> p c n", p=128), [], [wpg_r], "d_wpg", eng="pool")
                        hb2 = SB(ph, "hb2", [128, 8, SL], BF16)
                        hb2_r = Region("hb2", fence34)
                        sq2 = SB(ph, "sq2", [128, 8, SL], BF16)
                        sq2_r = Region("sq2", fence34)
                        rc2 = SB(ph, "rc2", [128, 4], F32)
                        rc2_r = Region("rc2", fence34)
                        rb2 = SB(ph, "rb2", [128, SL], F32)
                        rb2_r = Region("rb2", fence34)
                        dg4 = [SB(ph, f"dg4_{i}", [128, 128], F32) for i in range(2)]
                        dg4_r = [Region(f"dg4_{i}", fence34) for i in range(2)]
                        pst = [SB(ph, f"pst{i}", [128, 4, 256], BF16) for i in range(2)]
                        pst_r = [Region(f"pst{i}", fence34) for i in range(2)]
                        pT = SB(ph, "pT", [128, 2, SL], BF16)
                        pT_r = Region("pT", fence34)
                        u4 = [SB(ph, f"u4_{i}", [128, SL], F32) for i in range(2)]
                        u4_r = [Region(f"u4_{i}", fence34) for i in range(2)]
                        t4 = [SB(ph, f"t4_{i}", [128, SL], F32) for i in range(2)]
                        t4_r = [Region(f"t4_{i}", fence34) for i in range(2)]
                        zrot = Rot([0, 1, 4, 5])
                        yrot = Rot([0, 1, 7])
                        perot = Rot([4, 5])
                        def OPh(t, fcs):
                            sl = slice(t * SL, (t + 1) * SL)
                            pi = t % 2
                            if fcs[0] == 0:
                                dma(pst[pi][:, :, :], p_d[l, s, sl, :].rearrange("(i p) f -> p i f", p=128), [], [pst_r[pi]],
                                    f"d_pst{pi}", eng="pool")
                            for fc in fcs:
                                bk = zrot.next()
                                for kc in range(8):
                                    srcm = mixTa if kc < 4 else mixTb
                                    mm(PS[bk], wo[:, kc, fc * 128:(fc + 1) * 128], srcm[:, kc % 4, sl], kc == 0, kc == 7,
                                       [w4_r] + mixT_r[t * 4:(t + 1) * 4], [PR[bk]], kc == 7)
                                TT(hT[:, fc, sl], PS[bk], hT[:, fc, sl], ALU.add, [PR[bk], hT_r[t]], [hT_r[t]])

                        def STATS_A(t):
                            sl = slice(t * SL, (t + 1) * SL)
                            TT(hb2[:, :, :], hT[:, :, sl], bc_last(gple[:, l, :], SL), ALU.mult, [hT_r[t], par_r], [hb2_r])
                            ACT(sq2[:, :, :], hT[:, :, sl], AF.Square, [hT_r[t]], [sq2_r])

                        def STATS_N1(t):
                            norm_cols(lambda c, j: sq2[:, c, j * 128:(j + 1) * 128], [sq2_r], 8, 4, 0, 1.0 / DM, rc2, rc2_r)
                            rsqrt_cols(rc2[:, :], rc2_r)

                        def STATS_BC(t):
                            pi = t % 2
                            bcast_cols(rc2, rc2_r, 0, 4, dg4, dg4_r, rb2[:, :], rb2_r)
                            for kc in range(2):
                                for j in range(4):
                                    tr(PSB[6][:, kc * 512 + j * 128:kc * 512 + (j + 1) * 128],
                                       pst[pi][:, j, kc * 128:(kc + 1) * 128], ident_b[:], [pst_r[pi], const_r], [PR[6]],
                                       (kc == 1 and j == 3))
                            ACT(pT[:, :, :], PSB[6][:, 0:1024].rearrange("p (c t) -> p c t", t=SL), AF.Copy, [PR[6]], [pT_r])

                        def PLE(t):
                            sl = slice(t * SL, (t + 1) * SL)
                            pend4 = None

                            def fin4(fc, ui, bp):
                                STT(u4[ui][:], t4[ui][:], 1.0, PS[bp], ALU.add, ALU.mult, [t4_r[ui], PR[bp], u4_r[ui]], [u4_r[ui]])
                                STT(hT[:, fc, sl], u4[ui][:], 0.5, hT[:, fc, sl], ALU.mult, ALU.add, [u4_r[ui], hT_r[t]], [hT_r[t]])

                            for fc in range(8):
                                by = yrot.next()
                                for kc in range(8):
                                    mm(PS[by], wpg[:, kc, fc * 128:(fc + 1) * 128], hb2[:, kc, :], kc == 0, kc == 7,
                                       [wpg_r, hb2_r], [PR[by]], kc == 7)
                                bp = perot.next()
                                for kc in range(2):
                                    mm(PS[bp], wpe[:, kc, fc * 128:(fc + 1) * 128], pT[:, kc, :], kc == 0, kc == 1,
                                       [w4_r, pT_r], [PR[bp]], kc == 1)
                                ui = fc % 2
                                TT(u4[ui][:], PS[by], rb2[:, :], ALU.mult, [PR[by], rb2_r], [u4_r[ui]])
                                ACT(t4[ui][:], u4[ui][:], AF.Tanh, [u4_r[ui], par_r], [t4_r[ui]], bias=hbpg[:, l, fc:fc + 1],
                                    scale=0.5)
                                if pend4 is not None:
                                    fin4(*pend4)
                                pend4 = (fc, ui, bp)
                            fin4(*pend4)

                        OPh(0, list(range(8)))
                        for t in range(NSLAB):
                            STATS_A(t)
                            if t + 1 < NSLAB:
                                OPh(t + 1, [0, 1, 2, 3])
                            STATS_N1(t)
                            if t + 1 < NSLAB:
                                OPh(t + 1, [4, 5, 6, 7])
                            STATS_BC(t)
                            PLE(t)
                        nm = f"dbg_h{l + 1}"
                        if nm in dbg and s == 0:
                            dma(dbg[nm], hT[:, :, :], hT_r, [Region("d")], "d_dbg")
                        S.barrier()
                if stop_after == f"l{l}":
                    stopped = True
                    break
            if stopped:
                break

            with contextlib.ExitStack() as ph:
                gfin = SB(ph, "gfin", [128, DM], F32)
                gfin_r = Region("gfin")
                sqf = SB(ph, "sqf", [128, 8, SL], BF16)
                sqf_r = Region("sqf")
                rcf = SB(ph, "rcf", [128, 4], F32)
                rcf_r = Region("rcf")
                yst = [SB(ph, f"yst{i}", [128, DM], F32) for i in range(4)]
                yst_r = [Region(f"yst{i}") for i in range(4)]
                dma(gfin[:, :], gfin_d, [], [gfin_r], "d_gfin")
                frot = Rot([0, 1, 4, 5])
                for t in range(NSLAB):
                    sl = slice(t * SL, (t + 1) * SL)
                    ACT(sqf[:, :, :], hT[:, :, sl], AF.Square, [hT_r[t]], [sqf_r])
                    norm_cols(lambda c, j: sqf[:, c, j * 128:(j + 1) * 128], [sqf_r], 8, 4, 0, 1.0 / DM, rcf, rcf_r)
                    rsqrt_cols(rcf[:, :], rcf_r)
                    for j in range(4):
                        i = t * 4 + j
                        yi = i % 4
                        for g in range(2):
                            bk = frot.next()
                            for c4 in range(4):
                                c = g * 4 + c4
                                tr(PS[bk][:, c4 * 128:(c4 + 1) * 128], hT[:, c, i * 128:(i + 1) * 128], ident_f[:],
                                   [hT_r[t], const_r], [PR[bk]], c4 == 3)
                            STT(yst[yi][:, g * 512:(g + 1) * 512], PS[bk], rcf[:, j:j + 1], gfin[:, g * 512:(g + 1) * 512],
                                ALU.mult, ALU.mult, [PR[bk], rcf_r, gfin_r], [yst_r[yi]])
                        dma(y_d[s, i * 128:(i + 1) * 128, :], yst[yi][:, :], [yst_r[yi]], [Region("y")], f"d_yst{yi}")
                S.barrier()
        S.final_wait("sp")
        S.emit()
    return nc, S


def _host_inputs(x, p, positions, norm_mix, w_in, rel_bias, g_q, w_uq, g_kv, w_ukv, w_out,
                 norm_ple, w_pe, w_pg, b_pg, norm_final):
    f = lambda a: np.ascontiguousarray(np.asarray(a), dtype=np.float32)
    x, p = f(x), f(p)
    positions = np.ascontiguousarray(np.asarray(positions), dtype=np.int32)
    vec = lambda a, c: np.ascontiguousarray(f(a).reshape(2, c, 128).transpose(2, 0, 1))
    rb = f(rel_bias)
    shared = {
        "w_in": f(w_in), "w_uq": f(w_uq), "w_ukv": f(w_ukv), "w_out": f(w_out), "w_pe": f(w_pe), "w_pg": f(w_pg),
        "gmix": vec(norm_mix, 8), "gple": vec(norm_ple, 8), "bpg": vec(b_pg, 8), "gq": vec(g_q, 2), "gkv": vec(g_kv, 1),
        "cfar": np.ascontiguousarray(np.broadcast_to(rb[None, :, :, 256], (128, 2, 8))),
        "extb": np.ascontiguousarray(rb[:, :, 256 - np.maximum(np.arange(384) - 127, 0)]),
        "gfin": np.ascontiguousarray(np.broadcast_to(f(norm_final)[None, :], (128, DM))),
        "sgn": np.ascontiguousarray(np.where((np.arange(128) % 64) < 32, -1.0, 1.0).astype(np.float32).reshape(128, 1)),
        "invf": np.ascontiguousarray(np.tile((np.float32(10000.0) ** (-np.arange(0, 64, 2, dtype=np.float32) / np.float32(64)))
                                             .astype(np.float32), 4).reshape(128, 1)),
    }
    in_maps = []
    for c in range(NCORES):
        m = dict(shared)
        m["x"] = np.ascontiguousarray(x[2 * c:2 * c + 2])
        m["p"] = np.ascontiguousarray(p[:, 2 * c:2 * c + 2])
        m["pos"] = np.ascontiguousarray(positions[2 * c:2 * c + 2])
        in_maps.append(m)
    return in_maps


def kernel(x, p, positions, norm_mix, w_in, rel_bias, g_q, w_uq, g_kv, w_ukv, w_out,
           norm_ple, w_pe, w_pg, b_pg, norm_final):
    in_maps = _host_inputs(x, p, positions, norm_mix, w_in, rel_bias, g_q, w_uq, g_kv, w_ukv, w_out,
                           norm_ple, w_pe, w_pg, b_pg, norm_final)
    nc, _ = build_program()
    res = run_bass_kernel_spmd(nc, in_maps, core_ids=list(range(NCORES)))
    out = np.concatenate([np.asarray(r["y"], dtype=np.float32) for r in res.results], axis=0)
    return out
```

```python
import contextlib
import numpy as np
import concourse.bass as bass
import concourse.mybir as mybir
from concourse.bass_utils import run_bass_kernel_spmd

F32 = mybir.dt.float32
BF16 = mybir.dt.bfloat16
I32 = mybir.dt.int32
AF = mybir.ActivationFunctionType
ALU = mybir.AluOpType

NCORES = 8
SEQ = 2048
DM = 1024
NT = SEQ // 128
NSLAB = 4
SL = 512
EPS = 1e-6
SCALE_A = 64 ** -0.5
SCALE_B = 192 ** -0.5
TWO_PI_HI = 6.28125
TWO_PI_LO = 0.0019353071795864769


class Region:
    __slots__ = ("name", "w", "rs")

    def __init__(self, name, fence=None):
        self.name = name
        self.w = None
        self.rs = list(fence) if fence else []


class Sched:
    ENGS = ("pe", "act", "dve", "pool", "sp")

    def __init__(self, nc):
        self.nc = nc
        self.ops = {e: [] for e in self.ENGS}
        self.cnt = {}
        self.known = {e: {} for e in self.ENGS}
        self.sems = {}
        self.snap = {}
        self.nops = 0

    def sem(self, key):
        if key not in self.sems:
            self.sems[key] = self.nc.alloc_semaphore(f"s_{key}")
            self.cnt[key] = 0
        return self.sems[key]

    def _need(self, eng, deps):
        kn = self.known[eng]
        best = {}
        for (k, v) in deps:
            if kn.get(k, 0) >= v:
                continue
            if best.get(k, 0) < v:
                best[k] = v
        for k, v in best.items():
            kn[k] = v
            sn = self.snap.get((k, v))
            if sn:
                for kk, vv in sn.items():
                    if kn.get(kk, 0) < vv:
                        kn[kk] = vv
        return list(best.items())

    def op(self, eng, fn, reads=(), writes=(), inc=True, dma_key=None):
        deps = []
        for r in reads:
            if r.w is not None:
                deps.append(r.w)
        for w in writes:
            if w.w is not None and not (dma_key is not None and w.w[0] == dma_key):
                deps.append(w.w)
            deps.extend(w.rs)
        if eng == "pe":
            deps = [d for d in deps if d[0] != "pe"]
        waits = self._need(eng, deps)
        if dma_key is not None:
            key = dma_key
            self.sem(key)
            self.cnt[key] += 16
            val = self.cnt[key]
            incamt = 16
        else:
            key = eng
            self.sem(key)
            if inc:
                self.cnt[key] += 1
                val = self.cnt[key]
            else:
                val = self.cnt[key] + 1
            incamt = 1
        do_inc = inc or dma_key is not None
        self.ops[eng].append((fn, waits, (key, incamt) if do_inc else None))
        self.nops += 1
        tok = (key, val)
        sn = dict(self.known[eng])
        if dma_key is None:
            sn[key] = val
        self.snap[tok] = sn
        for r in reads:
            r.rs.append(tok)
        for w in writes:
            w.w = tok
            w.rs = []
        return tok

    def fence(self):
        return [(k, v) for k, v in self.cnt.items() if v > 0]

    def barrier(self):
        allk = [(k, v) for k, v in self.cnt.items() if v > 0]
        for e in self.ENGS:
            waits = self._need(e, allk)
            if waits:
                self.ops[e].append((None, waits, None))
        self.snap = {}

    def final_wait(self, eng):
        allk = [(k, v) for k, v in self.cnt.items() if v > 0]
        waits = self._need(eng, allk)
        self.ops[eng].append((None, waits, None))

    def emit(self):
        nc = self.nc
        handles = {"pe": "tensor", "act": "scalar", "dve": "vector", "pool": "gpsimd", "sp": "sync"}
        sems = self.sems
        with nc.Block() as block:
            for e in self.ENGS:
                ops = self.ops[e]

                def body(h, ops=ops):
                    for fn, waits, incd in ops:
                        for (k, v) in waits:
                            h.wait_ge(sems[k], v)
                        if fn is None:
                            continue
                        ins = fn(h)
                        if incd is not None:
                            ins.then_inc(sems[incd[0]], incd[1])
                getattr(block, handles[e])(body)


class Rot:
    def __init__(self, items):
        self.items = list(items)
        self.i = 0

    def next(self):
        it = self.items[self.i % len(self.items)]
        self.i += 1
        return it


def build_program(taps=()):
    nc = bass.Bass("TRN2", target_bir_lowering=False)
    S = Sched(nc)
    taps = set(taps)

    def din(name, shape, dt):
        return nc.dram_tensor(name, shape, dt, kind="ExternalInput").ap()

    def dscr(name, shape, dt):
        kind = "ExternalOutput" if name in taps else "Internal"
        return nc.dram_tensor(name, shape, dt, kind=kind).ap()

    x_d = din("x", [2, SEQ, DM], F32)
    p_d = din("p", [2, 2, SEQ, 256], F32)
    pos_d = din("pos", [2, SEQ], I32)
    w_in_d = din("w_in", [2, DM, 3008], F32)
    w_uq_d = din("w_uq", [2, 256, 768], F32)
    w_ukv_d = din("w_ukv", [2, 128, 1024], F32)
    w_out_d = din("w_out", [2, DM, DM], F32)
    w_pe_d = din("w_pe", [2, 256, DM], F32)
    w_pg_d = din("w_pg", [2, DM, DM], F32)
    gmix_d = din("gmix", [128, 2, 8], F32)
    gple_d = din("gple", [128, 2, 8], F32)
    bpg_d = din("bpg", [128, 2, 8], F32)
    gq_d = din("gq", [128, 2, 2], F32)
    gkv_d = din("gkv", [128, 2, 1], F32)
    cfar_d = din("cfar", [128, 2, 8], F32)
    extin_d = din("extb", [2, 8, 384], F32)
    gfin_d = din("gfin", [128, DM], F32)
    invf_d = din("invf", [128, 1], F32)
    sgn_d = din("sgn", [128, 1], F32)
    y_d = nc.dram_tensor("y", [2, SEQ, DM], F32, kind="ExternalOutput").ap()

    qTa_d = dscr("qTa", [128, NSLAB, 4, SL], BF16)
    kTa_d = dscr("kTa", [128, NSLAB, 4, SL], BF16)
    va_d = dscr("va", [128, NT, 8 * 65], BF16)
    sga_d = dscr("sga", [128, NT, 512], BF16)
    qb_d = dscr("qb", [128, NSLAB, 6, SL], BF16)
    kTb_d = dscr("kTb", [128, NSLAB, 4, SL], BF16)
    kr_d = dscr("kr", [128, SEQ], BF16)
    vb_d = dscr("vb", [128, NT, 4 * 129], BF16)
    sgb_d = dscr("sgb", [128, NT, 512], BF16)
    ropes_d = dscr("ropes", [2, 128, SEQ], F32)
    dbg = {}
    for nm, shp, dt in (("dbg_h0", [128, 8, SEQ], F32), ("dbg_h1", [128, 8, SEQ], F32), ("dbg_h2", [128, 8, SEQ], F32),
                        ("dbg_mixa", [128, 4, SEQ], BF16), ("dbg_mixb", [128, 4, SEQ], BF16),
                        ("dbg_E", [128, 8, 384], BF16), ("dbg_rope", [2, 128, SEQ], F32)):
        if nm in taps:
            dbg[nm] = nc.dram_tensor(nm, shp, dt, kind="ExternalOutput").ap()
    stop_after = None
    for tname in taps:
        if tname.startswith("stop:"):
            stop_after = tname[5:]

    qTa_r = [[Region(f"qTa{c}_{t}") for t in range(NSLAB)] for c in range(4)]
    kTa_r = [[Region(f"kTa{c}_{t}") for t in range(NSLAB)] for c in range(4)]
    va_r = [Region(f"va{i}") for i in range(NT)]
    sga_r = [Region(f"sga{i}") for i in range(NT)]
    qb_r = [[Region(f"qb{c}_{t}") for t in range(NSLAB)] for c in range(6)]
    kTb_r = [[Region(f"kTb{c}_{t}") for t in range(NSLAB)] for c in range(4)]
    kr_r = [Region(f"kr{t}") for t in range(NSLAB)]
    vb_r = [Region(f"vb{i}") for i in range(NT)]
    sgb_r = [Region(f"sgb{i}") for i in range(NT)]

    top = contextlib.ExitStack()

    sbn = [0]

    def SB(stack, name, shape, dt):
        sbn[0] += 1
        return stack.enter_context(nc.sbuf_tensor(f"sb{sbn[0]}_{name}", shape, dt))

    def ACT(out, in_, func, reads, writes, **kw):
        S.op("act", lambda h: h.activation(out=out, in_=in_, func=func, **kw), reads=reads, writes=writes)

    def TT(out, in0, in1, op, reads, writes, eng="dve"):
        S.op(eng, lambda h: h.tensor_tensor(out=out, in0=in0, in1=in1, op=op), reads=reads, writes=writes)

    def TS(out, in0, s1, op0, reads, writes, s2=None, op1=None, eng="dve"):
        if op1 is None:
            S.op(eng, lambda h: h.tensor_scalar(out=out, in0=in0, scalar1=s1, scalar2=None, op0=op0), reads=reads, writes=writes)
        else:
            S.op(eng, lambda h: h.tensor_scalar(out=out, in0=in0, scalar1=s1, scalar2=s2, op0=op0, op1=op1), reads=reads,
                 writes=writes)

    def STT(out, in0, scalar, in1, op0, op1, reads, writes, eng="dve"):
        S.op(eng, lambda h: h.scalar_tensor_tensor(out=out, in0=in0, scalar=scalar, in1=in1, op0=op0, op1=op1), reads=reads,
             writes=writes)

    def COPY(out, in_, reads, writes, eng="dve"):
        S.op(eng, lambda h: h.tensor_copy(out, in_), reads=reads, writes=writes)

    def RECIP(out, in_, reads, writes):
        S.op("dve", lambda h: h.reciprocal(out=out, in_=in_), reads=reads, writes=writes)

    def MEMSET(ap, val, reads, writes, eng="pool"):
        S.op(eng, lambda h: h.memset(ap, val), reads=reads, writes=writes)

    def dma(out, in_, reads, writes, key, eng="sp"):
        S.op(eng, lambda h: h.dma_start(out=out, in_=in_), reads=reads, writes=writes, dma_key=key)

    def mm(out, lhsT, rhs, start, stop, reads, writes, last, skip=False):
        S.op("pe", lambda h: h.matmul(out, lhsT=lhsT, rhs=rhs, start=start, stop=stop, skip_group_check=skip),
             reads=reads, writes=writes, inc=last)

    def tr(out, in_, ident, reads, writes, last):
        S.op("pe", lambda h: h.transpose(out, in_, ident), reads=reads, writes=writes, inc=last)

    def bc_last(ap2d, n):
        return bass.AP(ap2d.tensor, ap2d.offset, [list(ap2d.ap[0]), list(ap2d.ap[1]), [0, n]])

    with top:
        PSA = top.enter_context(nc.psum_tensor("psa", [128, 8 * 512], F32))
        PS = [PSA[:, i * 512:(i + 1) * 512] for i in range(8)]
        PSB = [PSA[:, i * 512:(i + 1) * 512].bitcast(BF16) for i in range(8)]
        PR = [Region(f"ps{i}") for i in range(8)]

        hT = SB(top, "hT", [128, 8, SEQ], F32)
        hT_r = [Region(f"hT{t}") for t in range(NSLAB)]
        ropes_r = Region("ropes_dram")
        sgn = SB(top, "sgn", [128, 1], F32)
        ident_f = SB(top, "ident_f", [128, 128], F32)
        ident_b = SB(top, "ident_b", [128, 128], BF16)
        ones_f = SB(top, "ones_f", [128, 128], F32)
        ones_b = SB(top, "ones_b", [128, 128], BF16)
        epsb = SB(top, "epsb", [128, 1], F32)
        const_r = Region("const")
        gmix = SB(top, "gmix", [128, 2, 8], F32)
        ngmix = SB(top, "ngmix", [128, 2, 8], F32)
        gple = SB(top, "gple", [128, 2, 8], F32)
        hbpg = SB(top, "hbpg", [128, 2, 8], F32)
        gq = SB(top, "gq", [128, 2, 2], F32)
        ngq = SB(top, "ngq", [128, 2, 2], F32)
        gkv = SB(top, "gkv", [128, 2, 1], F32)
        cneg = SB(top, "cneg", [128, 2, 8], F32)
        invf = SB(top, "invf", [128, 1], F32)
        par_r = Region("params")
        rcolP = SB(top, "rcolP", [128, NT], F32)
        rcolP_r = Region("rcolP")

        MEMSET(ident_f[:], 1.0, [], [const_r])
        S.op("pool", lambda h: h.affine_select(out=ident_f[:], in_=ident_f[:], pattern=[[-1, 128]],
                                               compare_op=ALU.is_equal, fill=0.0, base=0, channel_multiplier=1),
             reads=[const_r], writes=[const_r])
        MEMSET(ones_f[:], 1.0, [], [const_r])
        MEMSET(ones_b[:], 1.0, [], [const_r])
        MEMSET(epsb[:], EPS, [], [const_r])
        COPY(ident_b[:], ident_f[:], [const_r], [const_r])
        for (t_sb, t_d) in ((gmix, gmix_d), (gple, gple_d), (hbpg, bpg_d), (gq, gq_d), (gkv, gkv_d), (cneg, cfar_d)):
            dma(t_sb[:], t_d, [], [par_r], "d_par")
        dma(invf[:], invf_d, [], [par_r], "d_par")
        dma(sgn[:], sgn_d, [], [par_r], "d_par")
        TS(ngmix[:], gmix[:], -1.0, ALU.mult, [par_r], [par_r])
        TS(ngq[:], gq[:], -1.0, ALU.mult, [par_r], [par_r])
        TS(hbpg[:], hbpg[:], 0.5, ALU.mult, [par_r], [par_r])
        TS(cneg[:], cneg[:], -1.0, ALU.mult, [par_r], [par_r])
        Etl = [SB(top, f"Et{l}", [128, 8, 384], BF16) for l in range(2)]
        E_r = Region("E")
        S.barrier()

        def norm_cols(sq_ap, sq_reads, nchunk, n, col0, scale, rcol, rcol_r):
            for j in range(n):
                for c in range(nchunk):
                    mm(PS[2][:, col0 + j:col0 + j + 1], sq_ap(c, j), ones_b[:, 0:1], c == 0, c == nchunk - 1,
                       sq_reads + [const_r], [PR[2]], c == nchunk - 1)
            TS(rcol[:, col0:col0 + n], PS[2][:, col0:col0 + n], scale, ALU.mult, [PR[2]], [rcol_r])

        def rsqrt_cols(rc, rc_r):
            ACT(rc, rc, AF.Sqrt, [rc_r, const_r], [rc_r], bias=epsb[:, 0:1], scale=1.0)
            RECIP(rc, rc, [rc_r], [rc_r])

        def bcast_cols(rcol, rcol_r, j0, n, dg, dg_r, out_ap, out_r):
            for j in range(n):
                i = j % len(dg)
                TS(dg[i][:], ident_f[:], rcol[:, j0 + j:j0 + j + 1], ALU.mult, [rcol_r, const_r], [dg_r[i]])
                mm(PS[3][:, j * 128:(j + 1) * 128], ones_f[:], dg[i][:], True, True, [dg_r[i], const_r], [PR[3]], True)
            ACT(out_ap, PS[3][:, 0:n * 128], AF.Copy, [PR[3]], [out_r])

        stopped = False
        for s in range(2):
            if stopped:
                break
            with contextlib.ExitStack() as ph:
                xs = [SB(ph, f"xs{i}", [128, DM], F32) for i in range(4)]
                xs_r = [Region(f"xs{i}") for i in range(4)]
                xsq = SB(ph, "xsq", [128, DM], BF16)
                xsq_r = Region("xsq")
                posi = SB(ph, "posi", [128, SEQ], I32)
                angs = [SB(ph, f"ang{i}", [128, SEQ], F32) for i in range(2)]
                kf = SB(ph, "kf", [128, SEQ], F32)
                ki = SB(ph, "ki", [128, SEQ], I32)
                tmp_r = Region("ropetmp")
                cos4 = SB(ph, "cos4", [128, SEQ], F32)
                sin4 = SB(ph, "sin4", [128, SEQ], F32)
                rope_r = Region("rope")
                dma(posi[:], bass.AP(pos_d.tensor, pos_d[s, 0:1].offset, [[0, 128], [1, SEQ]]), [], [tmp_r], "d_pos")
                if s == 0:
                    tp = [SB(ph, f"tp{i}", [128, 128], F32) for i in range(32)]
                    tpall_r = Region("tpall")
                    for l_ in range(2):
                        for hh in range(8):
                            for bi, off in ((1, 0), (2, 128)):
                                i = l_ * 16 + hh * 2 + bi - 1
                                srcT = bass.AP(extin_d.tensor, extin_d[l_, hh, off:off + 1].offset, [[1, 128], [1, 128]])
                                dma(tp[i][:], srcT, [], [tpall_r], "d_tp", eng="pool")
                T = [tmp_r]
                for (ang, shift) in ((angs[0], 0.0), (angs[1], float(np.pi / 2))):
                    COPY(ang[:], posi[:], T, T)
                    TS(ang[:], ang[:], invf[:, 0:1], ALU.mult, T + [par_r], T)
                    if shift != 0.0:
                        TS(ang[:], ang[:], shift, ALU.add, T, T)
                    TS(kf[:], ang[:], float(1 / (2 * np.pi)), ALU.mult, T, T)
                    COPY(ki[:], kf[:], T, T)
                    COPY(kf[:], ki[:], T, T)
                    STT(ang[:], kf[:], -TWO_PI_HI, ang[:], ALU.mult, ALU.add, T, T)
                    STT(ang[:], kf[:], -TWO_PI_LO, ang[:], ALU.mult, ALU.add, T, T)
                    TS(kf[:], ang[:], float(np.pi), ALU.is_gt, T, T, s2=float(-2 * np.pi), op1=ALU.mult)
                    TT(ang[:], ang[:], kf[:], ALU.add, T, T)
                    TS(kf[:], ang[:], float(-np.pi), ALU.is_lt, T, T, s2=float(2 * np.pi), op1=ALU.mult)
                    TT(ang[:], ang[:], kf[:], ALU.add, T, T)
                prot = Rot([0, 1, 2, 3])
                for i in range(NT):
                    b = i % 4
                    dma(xs[b][:], x_d[s, i * 128:(i + 1) * 128, :], [], [xs_r[b]], f"d_xs{b}")
                    ACT(xsq[:], xs[b][:], AF.Square, [xs_r[b]], [xsq_r, rcolP_r], accum_out=rcolP[:, i:i + 1])
                    bk0 = prot.next()
                    bk1 = prot.next()
                    for g, bk in ((0, bk0), (1, bk1)):
                        for c4 in range(4):
                            c = g * 4 + c4
                            tr(PS[bk][:, c4 * 128:(c4 + 1) * 128], xs[b][:, c * 128:(c + 1) * 128], ident_f[:],
                               [xs_r[b], const_r], [PR[bk]], c4 == 3)
                    ACT(hT[:, :, i * 128:(i + 1) * 128], PSA[:, bk0 * 512:bk0 * 512 + 1024].rearrange("p (c t) -> p c t", t=128),
                        AF.Copy, [PR[bk0], PR[bk1]], [hT_r[i // 4]])
                if s == 0:
                    for l_ in range(2):
                        for hh in range(8):
                            MEMSET(Etl[l_][:, hh, 0:128], 1.0, [], [E_r])
                            for bi in (1, 2):
                                tv = tp[l_ * 16 + hh * 2 + bi - 1][:]
                                rev = bass.AP(tv.tensor, tv.offset + 127, [list(tv.ap[0]), [-1, 128]])
                                ACT(Etl[l_][:, hh, bi * 128:(bi + 1) * 128], rev, AF.Exp, [tpall_r, par_r], [E_r],
                                    bias=cneg[:, l_, hh:hh + 1], scale=1.0)
                    for l_ in range(2):
                        for hh in range(8):
                            MEMSET(Etl[l_][0:64, hh, 64:128], 0.0, [E_r], [E_r])
                            MEMSET(Etl[l_][64:128, hh, 256:320], 0.0, [E_r], [E_r])
                ACT(sin4[:], angs[0][:], AF.Sin, T, [rope_r])
                ACT(cos4[:], angs[1][:], AF.Sin, T, [rope_r])
                TS(sin4[:], sin4[:], sgn[:, 0:1], ALU.mult, [rope_r, par_r], [rope_r])
                dma(ropes_d[0], cos4[:], [rope_r], [ropes_r], "d_ropes")
                dma(ropes_d[1], sin4[:], [rope_r], [ropes_r], "d_ropes")
                if "dbg_rope" in dbg and s == 0:
                    dma(dbg["dbg_rope"][0], cos4[:], [rope_r], [Region("d")], "d_dbg")
                    dma(dbg["dbg_rope"][1], sin4[:], [rope_r], [Region("d")], "d_dbg")
                if "dbg_h0" in dbg and s == 0:
                    dma(dbg["dbg_h0"], hT[:, :, :], hT_r, [Region("d")], "d_dbg")
                S.barrier()
            if stop_after == "load":
                stopped = True
                break

            for l in range(2):
                with contextlib.ExitStack() as ph:
                    hbf = SB(ph, "hbf", [128, 8, SEQ], BF16)
                    hbf_r = [Region(f"hbf{t}") for t in range(NSLAB)]
                    sq = SB(ph, "sq", [128, 8, SL], BF16)
                    sq_r = Region("sq")
                    rcol = SB(ph, "rcol", [128, NT], F32)
                    rcol_r = Region("rcol")
                    rbc = SB(ph, "rbc", [128, SEQ], F32)
                    rbc_r = [Region(f"rbc{t}") for t in range(NSLAB)]
                    dg = [SB(ph, f"dg{i}", [128, 128], F32) for i in range(2)]
                    dg_r = [Region(f"dg{i}") for i in range(2)]
                    cos4 = SB(ph, "cos4", [128, SEQ], F32)
                    sin4 = SB(ph, "sin4", [128, SEQ], F32)
                    rope_r = Region("rope")
                    dma(cos4[:], ropes_d[0], [ropes_r], [rope_r], "d_ropel")
                    dma(sin4[:], ropes_d[1], [ropes_r], [rope_r], "d_ropel")
                    wg = [SB(ph, f"wg{i}", [128, 8, 640], BF16) for i in range(2)]
                    wg_r = [Region(f"wg{i}") for i in range(2)]
                    wq = SB(ph, "wq", [128, 2, 1024], BF16)
                    wkv = SB(ph, "wkv", [128, 1024], BF16)
                    wq_r = Region("wq")
                    ost = [SB(ph, f"ost{i}", [128, SL], BF16) for i in range(6)]
                    ost_r = [Region(f"ost{i}") for i in range(6)]
                    vaug = [SB(ph, f"vaug{i}", [128, 8, 65], BF16) for i in range(3)]
                    vaug_r = [Region(f"vaug{i}") for i in range(3)]
                    vbaug = [SB(ph, f"vbaug{i}", [128, 4, 129], BF16) for i in range(3)]
                    vbaug_r = [Region(f"vbaug{i}") for i in range(3)]
                    f32a = [SB(ph, f"f32a{i}", [128, SL], F32) for i in range(4)]
                    f32a_r = [Region(f"f32a{i}") for i in range(4)]
                    cq32 = SB(ph, "cq32", [128, 2, SL], F32)
                    ckv32 = SB(ph, "ckv32", [128, SL], F32)
                    c32_r = Region("c32")
                    sqq = SB(ph, "sqq", [128, 3, SL], BF16)
                    sqq_r = Region("sqq")
                    rqc = SB(ph, "rqc", [128, 8], F32)
                    rqc_r = Region("rqc")
                    rqb = SB(ph, "rqb", [128, 2, SL], F32)
                    rqb_r = Region("rqb")
                    cqn = SB(ph, "cqn", [128, 2, SL], BF16)
                    ckvn = SB(ph, "ckvn", [128, SL], BF16)
                    cn_r = Region("cn")

                    for i in range(3):
                        MEMSET(vaug[i][:], 1.0, [], [vaug_r[i]])
                        MEMSET(vbaug[i][:], 1.0, [], [vbaug_r[i]])

                    w_in3 = w_in_d[l].rearrange("(c p) n -> p c n", p=128)

                    def load_group(slot, c0, ncol, special=False):
                        key = f"d_wg{slot}"
                        if not special:
                            dma(wg[slot][:, :, 0:ncol], w_in3[:, :, c0:c0 + ncol], [], [wg_r[slot]], key, eng="pool")
                        else:
                            dma(wg[slot][:, :, 0:448], w_in3[:, :, c0:c0 + 448], [], [wg_r[slot]], key, eng="pool")
                            dma(wg[slot][:, :, 448:512], w_in3[:, :, c0 + 384:c0 + 448], [], [wg_r[slot]], key, eng="pool")
                            for d0 in (512, 576):
                                dma(wg[slot][:, :, d0:d0 + 32], w_in3[:, :, c0 + 416:c0 + 448], [], [wg_r[slot]], key, eng="pool")
                                dma(wg[slot][:, :, d0 + 32:d0 + 64], w_in3[:, :, c0 + 384:c0 + 416], [], [wg_r[slot]], key, eng="pool")

                    load_group(0, 0, 512)
                    load_group(1, 512, 512)
                    wuq3 = w_uq_d[l].rearrange("(kc p) (h d) -> p kc h d", p=128, d=192)
                    for kc in range(2):
                        dma(wq[:, kc, 0:512].rearrange("p (h d) -> p h d", d=128), wuq3[:, kc, :, 0:128], [], [wq_r], "d_wq",
                            eng="pool")
                    for kc in range(2):
                        for hh in range(4):
                            pp, half = hh % 2, hh // 2
                            ra = 512 + pp * 128 + half * 64
                            rb_ = 768 + pp * 128 + half * 64
                            dma(wq[:, kc, ra:ra + 64], wuq3[:, kc, hh, 128:192], [], [wq_r], "d_wq", eng="pool")
                            dma(wq[:, kc, rb_:rb_ + 32], wuq3[:, kc, hh, 160:192], [], [wq_r], "d_wq", eng="pool")
                            dma(wq[:, kc, rb_ + 32:rb_ + 64], wuq3[:, kc, hh, 128:160], [], [wq_r], "d_wq", eng="pool")
                    wkv4 = w_ukv_d[l].rearrange("p (h t d) -> p h t d", t=2, d=128)
                    dma(wkv[:, 0:512].rearrange("p (h d) -> p h d", d=128), wkv4[:, :, 0, :], [], [wq_r], "d_wq", eng="pool")
                    dma(wkv[:, 512:1024].rearrange("p (h d) -> p h d", d=128), wkv4[:, :, 1, :], [], [wq_r], "d_wq", eng="pool")


                    if l == 0:
                        TS(rcol[:, :], rcolP[:, :], 1.0 / DM, ALU.mult, [rcolP_r], [rcol_r])
                    for t in range(NSLAB):
                        sl = slice(t * SL, (t + 1) * SL)
                        TT(hbf[:, :, sl], hT[:, :, sl], bc_last(gmix[:, l, :], SL), ALU.mult, [hT_r[t], par_r], [hbf_r[t]])
                        if l > 0:
                            ACT(sq[:, :, :], hT[:, :, sl], AF.Square, [hT_r[t]], [sq_r])
                            norm_cols(lambda c, j: sq[:, c, j * 128:(j + 1) * 128], [sq_r], 8, 4, t * 4, 1.0 / DM, rcol, rcol_r)

                    def z_mm(slot, mc, t, bk):
                        for c in range(8):
                            mm(PS[bk], wg[slot][:, c, mc * 128:(mc + 1) * 128], hbf[:, c, t * SL:(t + 1) * SL],
                               c == 0, c == 7, [wg_r[slot], hbf_r[t]], [PR[bk]], c == 7)

                    pre_z = []
                    for (bk, mc, t) in ((0, 0, 0), (1, 0, 1), (4, 0, 2), (5, 0, 3), (6, 1, 0), (7, 1, 1)):
                        z_mm(0, mc, t, bk)
                        pre_z.append((bk, mc, t))
                    rsqrt_cols(rcol[:, :], rcol_r)
                    for t in range(NSLAB):
                        bcast_cols(rcol, rcol_r, t * 4, 4, dg, dg_r, rbc[:, t * SL:(t + 1) * SL], rbc_r[t])

                    zrot = Rot([0, 1, 4, 5])
                    orot = Rot(range(6))

                    def fm_group(slot, nmc, dst_d, dst_r, pre=()):
                        pre = list(pre)
                        for mc in range(nmc):
                            for t in range(NSLAB):
                                if pre and pre[0][1:] == (mc, t):
                                    bk = pre.pop(0)[0]
                                else:
                                    bk = zrot.next()
                                    z_mm(slot, mc, t, bk)
                                o = orot.next()
                                TT(ost[o][:], PS[bk], rbc[:, t * SL:(t + 1) * SL], ALU.mult, [PR[bk], rbc_r[t]], [ost_r[o]])
                                dma(dst_d[:, t, mc, :], ost[o][:], [ost_r[o]], [dst_r[mc][t]], f"d_ost{o}")

                    def gate_group(slot, dst_d, dst_r, tiles=range(NT)):
                        pendg = None

                        def fin(i, fu, ft):
                            o = orot.next()
                            STT(ost[o][:], f32a[ft][:], 1.0, f32a[fu][:], ALU.add, ALU.mult, [f32a_r[fu], f32a_r[ft]], [ost_r[o]])
                            dma(dst_d[:, i, :], ost[o][:], [ost_r[o]], [dst_r[i]], f"d_ost{o}")

                        for i in tiles:
                            bk = zrot.next()
                            for c in range(8):
                                mm(PS[bk], hbf[:, c, i * 128:(i + 1) * 128], wg[slot][:, c, 0:512], c == 0, c == 7,
                                   [wg_r[slot], hbf_r[i // 4]], [PR[bk]], c == 7)
                            fu, ft = (0, 1) if i % 2 == 0 else (2, 3)
                            TS(f32a[fu][:], PS[bk], rcol[:, i:i + 1], ALU.mult, [PR[bk], rcol_r], [f32a_r[fu]])
                            ACT(f32a[ft][:], f32a[fu][:], AF.Tanh, [f32a_r[fu]], [f32a_r[ft]], scale=0.5)
                            if pendg is not None:
                                fin(*pendg)
                            pendg = (i, fu, ft)
                        fin(*pendg)

                    fm_group(0, 4, qTa_d, qTa_r, pre_z)
                    load_group(0, 1024, 512)
                    fm_group(1, 4, kTa_d, kTa_r)
                    load_group(1, 1536, 512)
                    vrot = Rot(range(3))
                    for i in range(NT):
                        bk = zrot.next()
                        for c in range(8):
                            mm(PS[bk], hbf[:, c, i * 128:(i + 1) * 128], wg[0][:, c, 0:512], c == 0, c == 7,
                               [wg_r[0], hbf_r[i // 4]], [PR[bk]], c == 7)
                        v = vrot.next()
                        ACT(vaug[v][:, :, 0:64], PS[bk].rearrange("p (h d) -> p h d", d=64), AF.Copy, [PR[bk], rcol_r],
                            [vaug_r[v]], scale=rcol[:, i:i + 1])
                        dma(va_d[:, i, :], vaug[v][:, :, :].rearrange("p h d -> p (h d)"), [vaug_r[v]],
                            [va_r[i]], f"d_vaug{v}")
                    load_group(0, 2048, 448, special=True)
                    gate_group(1, sga_d, sga_r, range(0, NT // 2))
                    urot = Rot([0, 1, 4, 5, 6, 7])
                    vbrot = Rot(range(3))
                    def G4Z(t):
                        sl = slice(t * SL, (t + 1) * SL)
                        for mc in range(5):
                            bk = zrot.next() if mc < 3 else (6 if mc == 3 else 7)
                            for c in range(8):
                                mm(PS[bk], wg[0][:, c, mc * 128:(mc + 1) * 128], hbf[:, c, sl], c == 0, c == 7,
                                   [wg_r[0], hbf_r[t]], [PR[bk]], c == 7)
                            if mc < 2:
                                TT(cq32[:, mc, :], PS[bk], rbc[:, sl], ALU.mult, [PR[bk], rbc_r[t]], [c32_r])
                            elif mc == 2:
                                TT(ckv32[:, :], PS[bk], rbc[:, sl], ALU.mult, [PR[bk], rbc_r[t]], [c32_r])
                        TT(f32a[0][:], PS[6], cos4[:, sl], ALU.mult, [PR[6], rope_r], [f32a_r[0]])
                        TT(f32a[1][:], PS[7], sin4[:, sl], ALU.mult, [PR[7], rope_r], [f32a_r[1]])
                        TT(f32a[0][:], f32a[0][:], f32a[1][:], ALU.add, [f32a_r[0], f32a_r[1]], [f32a_r[0]])
                        o = orot.next()
                        TT(ost[o][:], f32a[0][:], rbc[:, sl], ALU.mult, [f32a_r[0], rbc_r[t]], [ost_r[o]])
                        dma(kr_d[:, sl], ost[o][:], [ost_r[o]], [kr_r[t]], f"d_ost{o}")

                    def G4CH(t):
                        sl = slice(t * SL, (t + 1) * SL)
                        ACT(sqq[:, 0:2, :], cq32[:, :, :], AF.Square, [c32_r], [sqq_r])
                        ACT(sqq[:, 2, :], ckv32[:, :], AF.Square, [c32_r], [sqq_r])
                        norm_cols(lambda c, j: sqq[:, c, j * 128:(j + 1) * 128], [sqq_r], 2, 4, 0, 1.0 / 256, rqc, rqc_r)
                        norm_cols(lambda c, j: sqq[:, 2, j * 128:(j + 1) * 128], [sqq_r], 1, 4, 4, 1.0 / 128, rqc, rqc_r)

                    def G4CH2(t):
                        rsqrt_cols(rqc[:, :], rqc_r)
                        bcast_cols(rqc, rqc_r, 0, 4, dg, dg_r, rqb[:, 0, :], rqb_r)
                        bcast_cols(rqc, rqc_r, 4, 4, dg, dg_r, rqb[:, 1, :], rqb_r)

                    def G4CH3(t):
                        for kc in range(2):
                            STT(cqn[:, kc, :], cq32[:, kc, :], gq[:, l, kc:kc + 1], rqb[:, 0, :], ALU.mult, ALU.mult,
                                [c32_r, rqb_r, par_r], [cn_r])
                        STT(ckvn[:, :], ckv32[:, :], gkv[:, l, 0:1], rqb[:, 1, :], ALU.mult, ALU.mult, [c32_r, rqb_r, par_r], [cn_r])

                    def G4UP(t):
                        sl = slice(t * SL, (t + 1) * SL)
                        for hh in range(4):
                            bk = urot.next()
                            for kc in range(2):
                                mm(PS[bk], wq[:, kc, hh * 128:(hh + 1) * 128], cqn[:, kc, :], kc == 0, kc == 1, [wq_r, cn_r],
                                   [PR[bk]], kc == 1)
                            o = orot.next()
                            ACT(ost[o][:], PS[bk], AF.Copy, [PR[bk]], [ost_r[o]])
                            dma(qb_d[:, t, hh, :], ost[o][:], [ost_r[o]], [qb_r[hh][t]], f"d_ost{o}")
                        for pp in range(2):
                            bA, bB = urot.next(), urot.next()
                            fa, fb = (0, 1) if pp == 0 else (2, 3)
                            for kc in range(2):
                                mm(PS[bA], wq[:, kc, 512 + pp * 128:640 + pp * 128], cqn[:, kc, :], kc == 0, kc == 1, [wq_r, cn_r],
                                   [PR[bA]], kc == 1)
                            for kc in range(2):
                                mm(PS[bB], wq[:, kc, 768 + pp * 128:896 + pp * 128], cqn[:, kc, :], kc == 0, kc == 1, [wq_r, cn_r],
                                   [PR[bB]], kc == 1)
                            TT(f32a[fa][:], PS[bA], cos4[:, sl], ALU.mult, [PR[bA], rope_r], [f32a_r[fa]])
                            TT(f32a[fb][:], PS[bB], sin4[:, sl], ALU.mult, [PR[bB], rope_r], [f32a_r[fb]])
                            o = orot.next()
                            TT(ost[o][:], f32a[fa][:], f32a[fb][:], ALU.add, [f32a_r[fa], f32a_r[fb]], [ost_r[o]])
                            dma(qb_d[:, t, 4 + pp, :], ost[o][:], [ost_r[o]], [qb_r[4 + pp][t]], f"d_ost{o}")

                    def G4UPkv(t):
                        sl = slice(t * SL, (t + 1) * SL)
                        for hh in range(4):
                            bk = urot.next()
                            mm(PS[bk], wkv[:, hh * 128:(hh + 1) * 128], ckvn[:, :], True, True, [wq_r, cn_r], [PR[bk]], True)
                            o = orot.next()
                            COPY(ost[o][:], PS[bk], [PR[bk]], [ost_r[o]])
                            dma(kTb_d[:, t, hh, :], ost[o][:], [ost_r[o]], [kTb_r[hh][t]], f"d_ost{o}")
                        for j in range(4):
                            i = t * 4 + j
                            bk = urot.next()
                            mm(PS[bk], ckvn[:, j * 128:(j + 1) * 128], wkv[:, 512:1024], True, True, [wq_r, cn_r], [PR[bk]], True)
                            v = vbrot.next()
                            ACT(vbaug[v][:, :, 0:128], PS[bk].rearrange("p (h d) -> p h d", d=128), AF.Copy, [PR[bk]],
                                [vbaug_r[v]])
                            dma(vb_d[:, i, :], vbaug[v][:, :, :].rearrange("p h d -> p (h d)"), [vbaug_r[v]],
                                [vb_r[i]], f"d_vbaug{v}")

                    G4Z(0)
                    G4CH(0)
                    G4CH2(0)
                    G4CH3(0)
                    gate_group(1, sga_d, sga_r, range(NT // 2, NT))
                    load_group(1, 2496, 512)
                    for t in range(NSLAB):
                        nxt = t + 1 < NSLAB
                        if nxt:
                            G4Z(t + 1)
                            G4CH(t + 1)
                        G4UP(t)
                        if nxt:
                            G4CH2(t + 1)
                        G4UPkv(t)
                        if nxt:
                            G4CH3(t + 1)
                    gate_group(1, sgb_d, sgb_r)
                    S.barrier()
                if stop_after == "p1":
                    stopped = True
                    break

                with contextlib.ExitStack() as ph23:
                    mixTa = SB(ph23, "mixTa", [128, 4, SEQ], BF16)
                    mixTb = SB(ph23, "mixTb", [128, 4, SEQ], BF16)
                    mixT_r = [Region(f"mixT{i}") for i in range(NT)]
                    with contextlib.ExitStack() as ph:
                        kT = SB(ph, "kT", [128, 4, SEQ], BF16)
                        kAB_r = [Region(f"kAB{t}") for t in range(NSLAB)]
                        vres = SB(ph, "vres", [128, NT, 8 * 65], BF16)
                        vres_r = [Region(f"vres{t}") for t in range(NSLAB)]
                        qtA = [SB(ph, f"qtA{i}", [128, 4, 128], BF16) for i in range(2)]
                        qtB = [SB(ph, f"qtB{i}", [128, 4, 128], BF16) for i in range(2)]
                        qt_r = [Region(f"qt{i}") for i in range(2)]
                        sgt = [SB(ph, f"sgt{i}", [128, 512], BF16) for i in range(2)]
                        sgt_r = [Region(f"sgt{i}") for i in range(2)]
                        PT = [SB(ph, f"PT{i}", [128, 640], BF16) for i in range(4)]
                        PT_r = [Region(f"PT{i}") for i in range(4)]
                        rcp = SB(ph, "rcp", [128, 8], F32)
                        rcp_r = Region("rcp")
                        on = SB(ph, "on", [128, 512], F32)
                        on_r = Region("on")
                        mx = [SB(ph, f"mx{i}", [128, 512], BF16) for i in range(2)]
                        mx_r = [Region(f"mx{i}") for i in range(2)]
                        def load_K(t):
                            sl = slice(t * SL, (t + 1) * SL)
                            dma(kT[:, :, sl], kTa_d[:, t, :, :], [kTa_r[c][t] for c in range(4)], [kAB_r[t]], f"d_kAB{t}")

                        def load_V(t):
                            dma(vres[:, t * 4:(t + 1) * 4, :], va_d[:, t * 4:(t + 1) * 4, :], va_r[t * 4:(t + 1) * 4], [vres_r[t]],
                                f"d_vres{t}")

                        def load_slab(t):
                            load_K(t)
                            load_V(t)

                        qtz_r = [Region(f"qtz{i}") for i in range(2)]
                        for i in range(2):
                            MEMSET(qtA[i][64:128, :, :], 0.0, [], [qtz_r[i]])
                            MEMSET(qtB[i][0:64, :, :], 0.0, [], [qtz_r[i]])
                        load_K(0)
                        Et = Etl[l]
                        SC = [0, 2, 4]
                        slotcol = {1: 0, 2: 128, 0: 256, 3: 384, 4: 512}
                        ptrot = Rot(range(4))
                        prev_tail = None
                        for pr in range(NT):
                            qb_ = pr % 2
                            qsrc = [qTa_r[c][pr // 4] for c in range(4)]
                            pc0 = (pr % 4) * 128
                            dma(qtA[qb_][0:64, :, :], qTa_d[0:64, pr // 4, :, pc0:pc0 + 128], qsrc, [qt_r[qb_]], f"d_qt{qb_}")
                            dma(qtB[qb_][64:128, :, :], qTa_d[64:128, pr // 4, :, pc0:pc0 + 128], qsrc, [qt_r[qb_]], f"d_qt{qb_}")
                            dma(sgt[qb_][:, :], sga_d[:, pr, :], [sga_r[pr]], [sgt_r[qb_]], f"d_sgt{qb_}")
                            blocks = [b for b in range(5) if pr - 4 + b >= 0]
                            if pr == 0:
                                load_V(0)
                                load_slab(1)
                            if pr % 4 == 1 and pr // 4 + 2 < NSLAB:
                                load_slab(pr // 4 + 2)

                            def scores(hh, sci):
                                c = hh // 2
                                qq = qtA if hh % 2 == 0 else qtB
                                b0 = SC[sci]
                                for b in blocks:
                                    j = pr - 4 + b
                                    col = b0 * 512 + slotcol[b]
                                    mm(PSA[:, col:col + 128], kT[:, c, j * 128:(j + 1) * 128], qq[qb_][:, c, :], True, True,
                                       [kAB_r[j // 4], qt_r[qb_], qtz_r[qb_]], [PR[b0], PR[b0 + 1]], b == blocks[-1])

                            def softmax_pv(hh, sci):
                                b0 = SC[sci]
                                pi = ptrot.next()
                                cols = sorted(slotcol[b] for b in blocks)
                                runs = []
                                for cc in cols:
                                    if runs and runs[-1][1] == cc:
                                        runs[-1][1] = cc + 128
                                    else:
                                        runs.append([cc, cc + 128])
                                for (c0, c1) in runs:
                                    ACT(PT[pi][:, c0:c1], PSA[:, b0 * 512 + c0:b0 * 512 + c1], AF.Exp, [PR[b0], PR[b0 + 1]],
                                        [PT_r[pi]], scale=SCALE_A)
                                if 0 in blocks:
                                    w0, e0 = 256, 0
                                elif 3 in blocks:
                                    w0, e0 = 384, 128
                                else:
                                    w0, e0 = 512, 256
                                TT(PT[pi][:, w0:640], PT[pi][:, w0:640], Et[:, hh, e0:384], ALU.mult, [PT_r[pi], E_r], [PT_r[pi]])
                                ob = 6 + hh // 4
                                oc = (hh % 4) * 65
                                for bi, b in enumerate(blocks):
                                    j = pr - 4 + b
                                    col = slotcol[b]
                                    mm(PS[ob][:, oc:oc + 65], PT[pi][:, col:col + 128], vres[:, j, hh * 65:(hh + 1) * 65],
                                       bi == 0, bi == len(blocks) - 1, [PT_r[pi], vres_r[j // 4]], [PR[ob]], bi == len(blocks) - 1)

                            pend = []
                            for hh in range(8):
                                scores(hh, hh % 3)
                                pend.append((hh, hh % 3))
                                if len(pend) > 2:
                                    softmax_pv(*pend.pop(0))
                            while pend:
                                softmax_pv(*pend.pop(0))
                            if prev_tail is not None:
                                prev_tail()
                            for g in range(2):
                                ov = PS[6 + g][:, 0:260].rearrange("p (h d) -> p h d", d=65)
                                RECIP(rcp[:, g * 4:(g + 1) * 4], ov[:, :, 64], [PR[6 + g]], [rcp_r])
                                TT(on[:, g * 256:(g + 1) * 256].rearrange("p (h d) -> p h d", d=64), ov[:, :, 0:64],
                                   bc_last(rcp[:, g * 4:(g + 1) * 4], 64), ALU.mult, [PR[6 + g], rcp_r], [on_r])
                            mxi = pr % 2
                            STT(mx[mxi][:, :], on[:, :], 0.5, sgt[qb_][:, :], ALU.mult, ALU.mult, [on_r, sgt_r[qb_]], [mx_r[mxi]])

                            def tail(pr=pr, mxi=mxi):
                                for c in range(4):
                                    tr(PSB[5][:, c * 128:(c + 1) * 128], mx[mxi][:, c * 128:(c + 1) * 128], ident_b[:],
                                       [mx_r[mxi], const_r], [PR[5]], c == 3)
                                ACT(mixTa[:, :, pr * 128:(pr + 1) * 128], PSB[5][:, 0:512].rearrange("p (c t) -> p c t", t=128),
                                    AF.Copy, [PR[5]], [mixT_r[pr]])
                            prev_tail = tail
                        if prev_tail is not None:
                            prev_tail()
                        if "dbg_mixa" in dbg and s == 0 and l == 0:
                            dma(dbg["dbg_mixa"], mixTa[:, :, :], mixT_r, [Region("d")], "d_dbg")
                        S.barrier()
                    if stop_after == "p2":
                        stopped = True
                        break
                    wo = SB(ph23, "wo", [128, 8, DM], BF16)
                    wpe = SB(ph23, "wpe", [128, 2, DM], BF16)
                    w4_r = Region("w4")

                    with contextlib.ExitStack() as ph:
                        kn = SB(ph, "kn", [128, 4, SEQ], BF16)
                        kn_r = [Region(f"kn{t}") for t in range(NSLAB)]
                        krT = SB(ph, "krT", [128, SEQ], BF16)
                        vbres = SB(ph, "vbres", [128, NT, 4 * 129], BF16)
                        vbres_r = [Region(f"vbres{t}") for t in range(NSLAB)]
                        qn = [SB(ph, f"qn{i}", [128, 8, SL], BF16) for i in range(2)]
                        qn_r = [Region(f"qn{i}") for i in range(2)]
                        sgb = [SB(ph, f"sgb{i}", [128, 4, 512], BF16) for i in range(2)]
                        sgb_sr = [Region(f"sgbs{i}") for i in range(2)]
                        PTb = [SB(ph, f"PTb{i}", [128, SL], BF16) for i in range(6)]
                        PTb_r = [Region(f"PTb{i}") for i in range(6)]
                        rcpb = SB(ph, "rcpb", [128, 4], F32)
                        rcpb_r = Region("rcpb")
                        onb = SB(ph, "onb", [128, 4, 128], F32)
                        onb_r = Region("onb")
                        mxb = [[SB(ph, f"mxb{k}_{i}", [128, 512], BF16) for i in range(4)] for k in range(2)]
                        mxb_r = [[Region(f"mxb{k}_{i}") for i in range(4)] for k in range(2)]
                        prev_tail_b = None
                        def load_kn_b(t):
                            sl = slice(t * SL, (t + 1) * SL)
                            dma(kn[:, :, sl], kTb_d[:, t, :, :], [kTb_r[c][t] for c in range(4)], [kn_r[t]], f"d_kn{t}")
                            dma(krT[:, sl], kr_d[:, sl], [kr_r[t]], [kn_r[t]], f"d_kn{t}")

                        def load_vb_b(t):
                            dma(vbres[:, t * 4:(t + 1) * 4, :], vb_d[:, t * 4:(t + 1) * 4, :], vb_r[t * 4:(t + 1) * 4], [vbres_r[t]],
                                f"d_vbres{t}")

                        def load_slab_b(t):
                            load_kn_b(t)
                            load_vb_b(t)
                        ptrot = Rot(range(6))
                        scrot = Rot([0, 1, 6, 7])
                        qz_r = [Region(f"qz{i}") for i in range(2)]
                        for i in range(2):
                            for pp in range(2):
                                MEMSET(qn[i][64:128, 4 + 2 * pp, :], 0.0, [], [qz_r[i]])
                                MEMSET(qn[i][0:64, 5 + 2 * pp, :], 0.0, [], [qz_r[i]])
                        def load_qpart(t):
                            qi = t % 2
                            qsrc = [qb_r[c][t] for c in range(6)]
                            dma(qn[qi][:, 0:4, :], qb_d[:, t, 0:4, :], qsrc, [qn_r[qi]], f"d_qn{qi}")
                            for pp in range(2):
                                dma(qn[qi][0:64, 4 + 2 * pp, :], qb_d[0:64, t, 4 + pp, :], qsrc, [qn_r[qi]], f"d_qn{qi}")
                                dma(qn[qi][64:128, 5 + 2 * pp, :], qb_d[64:128, t, 4 + pp, :], qsrc, [qn_r[qi]], f"d_qn{qi}")

                        def load_gate(t):
                            qi = t % 2
                            dma(sgb[qi][:, :, :], sgb_d[:, t * 4:(t + 1) * 4, :], [sgb_r[t * 4 + j] for j in range(4)], [sgb_sr[qi]],
                                f"d_sgb{qi}")

                        load_kn_b(0)
                        load_qpart(0)
                        load_vb_b(0)
                        load_qpart(1)
                        load_gate(0)
                        load_slab_b(1)
                        load_gate(1)
                        for t in range(NSLAB):
                            qi_ = t % 2
                            if t + 2 < NSLAB:
                                load_slab_b(t + 2)
                            if t >= 1 and t + 1 < NSLAB:
                                load_qpart(t + 1)
                                load_gate(t + 1)
                            if t == 0:
                                dma(wo[:, :, :], w_out_d[l].rearrange("(c p) n -> p c n", p=128), [], [w4_r], "d_w4", eng="pool")
                                dma(wpe[:, :, :], w_pe_d[l].rearrange("(c p) n -> p c n", p=128), [], [w4_r], "d_w4", eng="pool")
                            nkb = 4 * t + 4
                            for hp in range(2):
                                heads = (2 * hp, 2 * hp + 1)
                                obks = {hh: ((2, 3) if hh % 2 == 0 else (4, 5)) for hh in heads}
                                first_in_bank = {hh: {obks[hh][0]: True, obks[hh][1]: True} for hh in heads}

                                def scores_b(hh, j):
                                    pp, half = hh % 2, hh // 2
                                    r = max(0, j - 4 * t)
                                    c0 = r * 128
                                    bk = scrot.next()
                                    mm(PS[bk][:, c0:SL], kn[:, hh, j * 128:(j + 1) * 128], qn[qi_][:, hh, c0:SL], True, False,
                                       [kn_r[j // 4], qn_r[qi_]], [PR[bk]], False)
                                    mm(PS[bk][:, c0:SL], krT[:, j * 128:(j + 1) * 128], qn[qi_][:, 4 + 2 * pp + half, c0:SL], False, True,
                                       [kn_r[j // 4], qn_r[qi_], qz_r[qi_]], [PR[bk]], True)
                                    return (hh, j, bk, c0)

                                def pv_b(hh, j, bk, c0):
                                    pi = ptrot.next()
                                    ACT(PTb[pi][:, c0:SL], PS[bk][:, c0:SL], AF.Exp, [PR[bk]], [PTb_r[pi]], scale=SCALE_B)
                                    if j >= 4 * t:
                                        MEMSET(PTb[pi][64:128, c0:c0 + 64], 0.0, [PTb_r[pi]], [PTb_r[pi]])
                                    for qi in range(c0 // 128, 4):
                                        ob = obks[hh][qi // 2]
                                        oc = (qi % 2) * 129
                                        st = first_in_bank[hh][ob]
                                        first_in_bank[hh][ob] = False
                                        mm(PS[ob][:, oc:oc + 129], PTb[pi][:, qi * 128:(qi + 1) * 128],
                                           vbres[:, j, hh * 129:(hh + 1) * 129], st, j == 4 * t + qi, [PTb_r[pi], vbres_r[j // 4]],
                                           [PR[ob]], qi == 3, skip=True)

                                pend = {hh: None for hh in heads}
                                for j in range(nkb):
                                    cur = {hh: scores_b(hh, j) for hh in heads}
                                    for hh in heads:
                                        if pend[hh] is not None:
                                            pv_b(*pend[hh])
                                        pend[hh] = cur[hh]
                                for hh in heads:
                                    pv_b(*pend[hh])
                                if hp == 0 and prev_tail_b is not None:
                                    prev_tail_b()
                                    prev_tail_b = None
                                for hh in heads:
                                    obk = obks[hh]
                                    for g in range(2):
                                        ov = PS[obk[g]][:, 0:258].rearrange("p (q d) -> p q d", d=129)
                                        RECIP(rcpb[:, g * 2:(g + 1) * 2], ov[:, :, 128], [PR[obk[g]]], [rcpb_r])
                                        TT(onb[:, g * 2:(g + 1) * 2, :], ov[:, :, 0:128], bc_last(rcpb[:, g * 2:(g + 1) * 2], 128),
                                           ALU.mult, [PR[obk[g]], rcpb_r], [onb_r])
                                    for qi in range(4):
                                        STT(mxb[t % 2][qi][:, hh * 128:(hh + 1) * 128], onb[:, qi, :], 0.5,
                                            sgb[qi_][:, qi, hh * 128:(hh + 1) * 128], ALU.mult, ALU.mult, [onb_r, sgb_sr[qi_]],
                                            [mxb_r[t % 2][qi]])

                            def tail_b(t=t):
                                for qi in range(4):
                                    i = t * 4 + qi
                                    bq = 6 + (qi % 2)
                                    for c in range(4):
                                        tr(PSB[bq][:, c * 128:(c + 1) * 128], mxb[t % 2][qi][:, c * 128:(c + 1) * 128], ident_b[:],
                                           [mxb_r[t % 2][qi], const_r], [PR[bq]], c == 3)
                                    ACT(mixTb[:, :, i * 128:(i + 1) * 128], PSB[bq][:, 0:512].rearrange("p (c t) -> p c t", t=128),
                                        AF.Copy, [PR[bq]], [mixT_r[i]])
                            prev_tail_b = tail_b
                        if prev_tail_b is not None:
                            prev_tail_b()
                        if "dbg_mixb" in dbg and s == 0 and l == 0:
                            dma(dbg["dbg_mixb"], mixTb[:, :, :], mixT_r, [Region("d")], "d_dbg")
                        fence34 = S.fence()
                    if stop_after == "p3":
                        stopped = True
                        break

                    with contextlib.ExitStack() as ph:
                        wpg = SB(ph, "wpg", [128, 8, DM], BF16)
                        wpg_r = Region("wpg", fence34)
                        dma(wpg[:, :, :], w_pg_d[l].rearrange("(c p) n -> p c n", p=128), [], [wpg_r], "d_wpg", eng="pool")
                        hb2 = SB(ph, "hb2", [128, 8, SL], BF16)
                        hb2_r = Region("hb2", fence34)
                        sq2 = SB(ph, "sq2", [128, 8, SL], BF16)
                        sq2_r = Region("sq2", fence34)
                        rc2 = SB(ph, "rc2", [128, 4], F32)
                        rc2_r = Region("rc2", fence34)
                        rb2 = SB(ph, "rb2", [128, SL], F32)
                        rb2_r = Region("rb2", fence34)
                        dg4 = [SB(ph, f"dg4_{i}", [128, 128], F32) for i in range(2)]
                        dg4_r = [Region(f"dg4_{i}", fence34) for i in range(2)]
                        pst = [SB(ph, f"pst{i}", [128, 4, 256], BF16) for i in range(2)]
                        pst_r = [Region(f"pst{i}", fence34) for i in range(2)]
                        pT = SB(ph, "pT", [128, 2, SL], BF16)
                        pT_r = Region("pT", fence34)
                        u4 = [SB(ph, f"u4_{i}", [128, SL], F32) for i in range(2)]
                        u4_r = [Region(f"u4_{i}", fence34) for i in range(2)]
                        t4 = [SB(ph, f"t4_{i}", [128, SL], F32) for i in range(2)]
                        t4_r = [Region(f"t4_{i}", fence34) for i in range(2)]
                        zrot = Rot([0, 1, 4, 5])
                        yrot = Rot([0, 1, 7])
                        perot = Rot([4, 5])
                        def OPh(t, fcs):
                            sl = slice(t * SL, (t + 1) * SL)
                            pi = t % 2
                            if fcs[0] == 0:
                                dma(pst[pi][:, :, :], p_d[l, s, sl, :].rearrange("(i p) f -> p i f", p=128), [], [pst_r[pi]],
                                    f"d_pst{pi}", eng="pool")
                            for fc in fcs:
                                bk = zrot.next()
                                for kc in range(8):
                                    srcm = mixTa if kc < 4 else mixTb
                                    mm(PS[bk], wo[:, kc, fc * 128:(fc + 1) * 128], srcm[:, kc % 4, sl], kc == 0, kc == 7,
                                       [w4_r] + mixT_r[t * 4:(t + 1) * 4], [PR[bk]], kc == 7)
                                TT(hT[:, fc, sl], PS[bk], hT[:, fc, sl], ALU.add, [PR[bk], hT_r[t]], [hT_r[t]])

                        def STATS_A(t):
                            sl = slice(t * SL, (t + 1) * SL)
                            TT(hb2[:, :, :], hT[:, :, sl], bc_last(gple[:, l, :], SL), ALU.mult, [hT_r[t], par_r], [hb2_r])
                            ACT(sq2[:, :, :], hT[:, :, sl], AF.Square, [hT_r[t]], [sq2_r])

                        def STATS_N1(t):
                            norm_cols(lambda c, j: sq2[:, c, j * 128:(j + 1) * 128], [sq2_r], 8, 4, 0, 1.0 / DM, rc2, rc2_r)
                            rsqrt_cols(rc2[:, :], rc2_r)

                        def STATS_BC(t):
                            pi = t % 2
                            bcast_cols(rc2, rc2_r, 0, 4, dg4, dg4_r, rb2[:, :], rb2_r)
                            for kc in range(2):
                                for j in range(4):
                                    tr(PSB[6][:, kc * 512 + j * 128:kc * 512 + (j + 1) * 128],
                                       pst[pi][:, j, kc * 128:(kc + 1) * 128], ident_b[:], [pst_r[pi], const_r], [PR[6]],
                                       (kc == 1 and j == 3))
                            ACT(pT[:, :, :], PSB[6][:, 0:1024].rearrange("p (c t) -> p c t", t=SL), AF.Copy, [PR[6]], [pT_r])

                        def PLE(t):
                            sl = slice(t * SL, (t + 1) * SL)
                            pend4 = None

                            def fin4(fc, ui, bp):
                                STT(u4[ui][:], t4[ui][:], 1.0, PS[bp], ALU.add, ALU.mult, [t4_r[ui], PR[bp], u4_r[ui]], [u4_r[ui]])
                                STT(hT[:, fc, sl], u4[ui][:], 0.5, hT[:, fc, sl], ALU.mult, ALU.add, [u4_r[ui], hT_r[t]], [hT_r[t]])

                            for fc in range(8):
                                by = yrot.next()
                                for kc in range(8):
                                    mm(PS[by], wpg[:, kc, fc * 128:(fc + 1) * 128], hb2[:, kc, :], kc == 0, kc == 7,
                                       [wpg_r, hb2_r], [PR[by]], kc == 7)
                                bp = perot.next()
                                for kc in range(2):
                                    mm(PS[bp], wpe[:, kc, fc * 128:(fc + 1) * 128], pT[:, kc, :], kc == 0, kc == 1,
                                       [w4_r, pT_r], [PR[bp]], kc == 1)
                                ui = fc % 2
                                TT(u4[ui][:], PS[by], rb2[:, :], ALU.mult, [PR[by], rb2_r], [u4_r[ui]])
                                ACT(t4[ui][:], u4[ui][:], AF.Tanh, [u4_r[ui], par_r], [t4_r[ui]], bias=hbpg[:, l, fc:fc + 1],
                                    scale=0.5)
                                if pend4 is not None:
                                    fin4(*pend4)
                                pend4 = (fc, ui, bp)
                            fin4(*pend4)

                        OPh(0, list(range(8)))
                        for t in range(NSLAB):
                            STATS_A(t)
                            if t + 1 < NSLAB:
                                OPh(t + 1, [0, 1, 2, 3])
                            STATS_N1(t)
                            if t + 1 < NSLAB:
                                OPh(t + 1, [4, 5, 6, 7])
                            STATS_BC(t)
                            PLE(t)
                        nm = f"dbg_h{l + 1}"
                        if nm in dbg and s == 0:
                            dma(dbg[nm], hT[:, :, :], hT_r, [Region("d")], "d_dbg")
                        S.barrier()
                if stop_after == f"l{l}":
                    stopped = True
                    break
            if stopped:
                break

            with contextlib.ExitStack() as ph:
                gfin = SB(ph, "gfin", [128, DM], F32)
                gfin_r = Region("gfin")
                sqf = SB(ph, "sqf", [128, 8, SL], BF16)
                sqf_r = Region("sqf")
                rcf = SB(ph, "rcf", [128, 4], F32)
                rcf_r = Region("rcf")
                yst = [SB(ph, f"yst{i}", [128, DM], F32) for i in range(4)]
                yst_r = [Region(f"yst{i}") for i in range(4)]
                dma(gfin[:, :], gfin_d, [], [gfin_r], "d_gfin")
                frot = Rot([0, 1, 4, 5])
                for t in range(NSLAB):
                    sl = slice(t * SL, (t + 1) * SL)
                    ACT(sqf[:, :, :], hT[:, :, sl], AF.Square, [hT_r[t]], [sqf_r])
                    norm_cols(lambda c, j: sqf[:, c, j * 128:(j + 1) * 128], [sqf_r], 8, 4, 0, 1.0 / DM, rcf, rcf_r)
                    rsqrt_cols(rcf[:, :], rcf_r)
                    for j in range(4):
                        i = t * 4 + j
                        yi = i % 4
                        for g in range(2):
                            bk = frot.next()
                            for c4 in range(4):
                                c = g * 4 + c4
                                tr(PS[bk][:, c4 * 128:(c4 + 1) * 128], hT[:, c, i * 128:(i + 1) * 128], ident_f[:],
                                   [hT_r[t], const_r], [PR[bk]], c4 == 3)
                            STT(yst[yi][:, g * 512:(g + 1) * 512], PS[bk], rcf[:, j:j + 1], gfin[:, g * 512:(g + 1) * 512],
                                ALU.mult, ALU.mult, [PR[bk], rcf_r, gfin_r], [yst_r[yi]])
                        dma(y_d[s, i * 128:(i + 1) * 128, :], yst[yi][:, :], [yst_r[yi]], [Region("y")], f"d_yst{yi}")
                S.barrier()
        S.final_wait("sp")
        S.emit()
    return nc, S


def _host_inputs(x, p, positions, norm_mix, w_in, rel_bias, g_q, w_uq, g_kv, w_ukv, w_out,
                 norm_ple, w_pe, w_pg, b_pg, norm_final):
    f = lambda a: np.ascontiguousarray(np.asarray(a), dtype=np.float32)
    x, p = f(x), f(p)
    positions = np.ascontiguousarray(np.asarray(positions), dtype=np.int32)
    vec = lambda a, c: np.ascontiguousarray(f(a).reshape(2, c, 128).transpose(2, 0, 1))
    rb = f(rel_bias)
    shared = {
        "w_in": f(w_in), "w_uq": f(w_uq), "w_ukv": f(w_ukv), "w_out": f(w_out), "w_pe": f(w_pe), "w_pg": f(w_pg),
        "gmix": vec(norm_mix, 8), "gple": vec(norm_ple, 8), "bpg": vec(b_pg, 8), "gq": vec(g_q, 2), "gkv": vec(g_kv, 1),
        "cfar": np.ascontiguousarray(np.broadcast_to(rb[None, :, :, 256], (128, 2, 8))),
        "extb": np.ascontiguousarray(rb[:, :, 256 - np.maximum(np.arange(384) - 127, 0)]),
        "gfin": np.ascontiguousarray(np.broadcast_to(f(norm_final)[None, :], (128, DM))),
        "sgn": np.ascontiguousarray(np.where((np.arange(128) % 64) < 32, -1.0, 1.0).astype(np.float32).reshape(128, 1)),
        "invf": np.ascontiguousarray(np.tile((np.float32(10000.0) ** (-np.arange(0, 64, 2, dtype=np.float32) / np.float32(64)))
                                             .astype(np.float32), 4).reshape(128, 1)),
    }
    in_maps = []
    for c in range(NCORES):
        m = dict(shared)
        m["x"] = np.ascontiguousarray(x[2 * c:2 * c + 2])
        m["p"] = np.ascontiguousarray(p[:, 2 * c:2 * c + 2])
        m["pos"] = np.ascontiguousarray(positions[2 * c:2 * c + 2])
        in_maps.append(m)
    return in_maps


def kernel(x, p, positions, norm_mix, w_in, rel_bias, g_q, w_uq, g_kv, w_ukv, w_out,
           norm_ple, w_pe, w_pg, b_pg, norm_final):
    in_maps = _host_inputs(x, p, positions, norm_mix, w_in, rel_bias, g_q, w_uq, g_kv, w_ukv, w_out,
                           norm_ple, w_pe, w_pg, b_pg, norm_final)
    nc, _ = build_program()
    res = run_bass_kernel_spmd(nc, in_maps, core_ids=list(range(NCORES)))
    out = np.concatenate([np.asarray(r["y"], dtype=np.float32) for r in res.results], axis=0)
    return out
```
